# Optimizing a Trainium2 kernel written in Bass

```python
import jax, jax.numpy as jnp
from jax import lax
import numpy as np

D_MODEL = 1024
BATCH = 8
SEQ = 2048
DEPTH = 4

GRID_W = 64
CTX_LEN = 256
N_MIXERS = 3
Q_BLOCK = 128
ROPE_THETA = 10000.0
EPS = 1e-6
N_MOD = 6
FFN_HIDDEN = -(-(8 * D_MODEL) // (3 * 256)) * 256
CONV_WIDTH = 3
GQA_HEAD_DIM = 128
GQA_HEADS = D_MODEL // GQA_HEAD_DIM
GQA_KV_HEADS = max(GQA_HEADS // 4, 1)
GQA_GROUP = GQA_HEADS // GQA_KV_HEADS
GQA_SCALE = GQA_HEAD_DIM ** -0.5
MLA_HEADS = D_MODEL // 128
MLA_NOPE = 128
MLA_ROPE = 64
MLA_V = 128
MLA_KV_RANK = D_MODEL // 4
MLA_Q_RANK = 3 * MLA_KV_RANK
MLA_SCALE = (MLA_NOPE + MLA_ROPE) ** -0.5
N_A = (DEPTH + 2) // 3
N_B = (DEPTH + 1) // 3
N_C = DEPTH // 3

kernel_name = 'hybrid_diffusion_backbone'


def rmsnorm(x, g):
    x32 = x.astype(jnp.float32)
    y = x32 * lax.rsqrt(jnp.mean(x32 * x32, axis=-1, keepdims=True) + EPS)
    return (y * g.astype(jnp.float32)).astype(x.dtype)


def modulation(cond_act, w, b):
    m = cond_act @ w + b
    return jnp.split(m[:, None, :], N_MOD, axis=-1)


def modulate(x, g, shift, scale):
    return rmsnorm(x, g) * (1 + scale) + shift


def swiglu(h, w1, w3, w2):
    return (jax.nn.silu(h @ w1) * (h @ w3)) @ w2


def axial_angles(rows, cols, rot_dim):
    n = rot_dim // 4
    freqs = ROPE_THETA ** (-jnp.arange(n, dtype=jnp.float32) / n)
    return jnp.concatenate([rows[:, None] * freqs, cols[:, None] * freqs], axis=-1)


def apply_rope(x, ang):
    half = ang.shape[-1]
    shape = (1, ang.shape[0]) + (1,) * (x.ndim - 3) + (half,)
    cos = jnp.cos(ang).reshape(shape)
    sin = jnp.sin(ang).reshape(shape)
    x32 = x.astype(jnp.float32)
    x1, x2 = x32[..., :half], x32[..., half:]
    return jnp.concatenate([x1 * cos - x2 * sin, x1 * sin + x2 * cos], axis=-1).astype(x.dtype)


def attend(q, k, v, scale):
    s = jnp.einsum('bqkgd,btkd->bkgqt', q, k, preferred_element_type=jnp.float32) * scale
    p = jax.nn.softmax(s, axis=-1).astype(v.dtype)
    return jnp.einsum('bkgqt,btkd->bqkgd', p, v)


def blocked_attend(q, k, v, scale):
    B, S = q.shape[0], q.shape[1]
    nb = S // Q_BLOCK
    qb = jnp.moveaxis(q.reshape((B, nb, Q_BLOCK) + q.shape[2:]), 1, 0)
    out = lax.map(lambda qq: attend(qq, k, v, scale), qb)
    out = jnp.moveaxis(out, 0, 1)
    return out.reshape((B, S) + out.shape[3:])


def short_conv_mixer(h, w_in, conv_w, w_out):
    S = h.shape[1]
    b_gate, c_gate, xv = jnp.split(h @ w_in, 3, axis=-1)
    pad = CONV_WIDTH // 2
    u = jnp.pad(c_gate * xv, ((0, 0), (pad, pad), (0, 0)))
    z = u[:, 0:S] * conv_w[0]
    for k in range(1, CONV_WIDTH):
        z = z + u[:, k:k + S] * conv_w[k]
    return (b_gate * z) @ w_out


def gqa_mixer(h_ctx, h_lat, wq, wk, wv, q_norm_g, k_norm_g, wo, ang, ctx_out):
    def proj(h):
        B, S, _ = h.shape
        q = (h @ wq).reshape(B, S, GQA_KV_HEADS, GQA_GROUP, GQA_HEAD_DIM)
        k = (h @ wk).reshape(B, S, GQA_KV_HEADS, GQA_HEAD_DIM)
        v = (h @ wv).reshape(B, S, GQA_KV_HEADS, GQA_HEAD_DIM)
        return rmsnorm(q, q_norm_g), rmsnorm(k, k_norm_g), v
    q_c, k_c, v_c = proj(h_ctx)
    q_l, k_l, v_l = proj(h_lat)
    q_l = apply_rope(q_l, ang)
    k_l = apply_rope(k_l, ang)
    k_all = jnp.concatenate([k_c, k_l], axis=1)
    v_all = jnp.concatenate([v_c, v_l], axis=1)
    B, S = h_lat.shape[0], h_lat.shape[1]
    y_l = blocked_attend(q_l, k_all, v_all, GQA_SCALE).reshape(B, S, GQA_HEADS * GQA_HEAD_DIM) @ wo
    y_c = None
    if ctx_out:
        L = h_ctx.shape[1]
        y_c = attend(q_c, k_c, v_c, GQA_SCALE).reshape(B, L, GQA_HEADS * GQA_HEAD_DIM) @ wo
    return y_c, y_l


def mla_project(h, w_dq, q_norm_g, w_uq, w_dkv, kv_norm_g, w_ukv, ang):
    B, S, _ = h.shape
    cq = rmsnorm(h @ w_dq, q_norm_g)
    q = (cq @ w_uq).reshape(B, S, MLA_HEADS, MLA_NOPE + MLA_ROPE)
    q_nope, q_pe = q[..., :MLA_NOPE], q[..., MLA_NOPE:]
    ckv_pe = h @ w_dkv
    ckv = rmsnorm(ckv_pe[..., :MLA_KV_RANK], kv_norm_g)
    k_pe = ckv_pe[..., None, MLA_KV_RANK:]
    kv = (ckv @ w_ukv).reshape(B, S, MLA_HEADS, MLA_NOPE + MLA_V)
    k_nope, v = kv[..., :MLA_NOPE], kv[..., MLA_NOPE:]
    if ang is not None:
        q_pe = apply_rope(q_pe, ang)
        k_pe = apply_rope(k_pe, ang)
    q = jnp.concatenate([q_nope, q_pe], axis=-1)[:, :, :, None, :]
    k = jnp.concatenate([k_nope, jnp.broadcast_to(k_pe, (B, S, MLA_HEADS, MLA_ROPE))], axis=-1)
    return q, k, v


def mla_mixer(h_ctx, h_lat, w_dq, q_norm_g, w_uq, w_dkv, kv_norm_g, w_ukv, wo, ang, ctx_out):
    q_c, k_c, v_c = mla_project(h_ctx, w_dq, q_norm_g, w_uq, w_dkv, kv_norm_g, w_ukv, None)
    q_l, k_l, v_l = mla_project(h_lat, w_dq, q_norm_g, w_uq, w_dkv, kv_norm_g, w_ukv, ang)
    k_all = jnp.concatenate([k_c, k_l], axis=1)
    v_all = jnp.concatenate([v_c, v_l], axis=1)
    B, S = h_lat.shape[0], h_lat.shape[1]
    y_l = blocked_attend(q_l, k_all, v_all, MLA_SCALE).reshape(B, S, MLA_HEADS * MLA_V) @ wo
    y_c = None
    if ctx_out:
        L = h_ctx.shape[1]
        y_c = attend(q_c, k_c, v_c, MLA_SCALE).reshape(B, L, MLA_HEADS * MLA_V) @ wo
    return y_c, y_l


def setup_inputs(seed: int = 0) -> dict:
    key = jax.random.key(seed)
    ks = iter(jax.random.split(key, 40))
    D = D_MODEL
    f32 = jnp.float32

    def w(shape, fan_in, mult=1.0):
        return jax.random.normal(next(ks), shape, f32) * (mult * fan_in ** -0.5)

    def gain(shape):
        return 1.0 + 0.02 * jax.random.normal(next(ks), shape, f32)

    return {
        'x': jax.random.normal(next(ks), (BATCH, SEQ, D), f32),
        'c': jax.random.normal(next(ks), (BATCH, D), f32),
        'ctx': jax.random.normal(next(ks), (BATCH, CTX_LEN, D), f32),
        'c_ctx': jax.random.normal(next(ks), (D,), f32),
        'ada_w': w((DEPTH, D, N_MOD * D), D, 0.5),
        'ada_b': 0.02 * jax.random.normal(next(ks), (DEPTH, N_MOD * D), f32),
        'norm1_g': gain((DEPTH, D)),
        'norm2_g': gain((DEPTH, D)),
        'ffn_w1': w((DEPTH, D, FFN_HIDDEN), D),
        'ffn_w3': w((DEPTH, D, FFN_HIDDEN), D),
        'ffn_w2': w((DEPTH, FFN_HIDDEN, D), FFN_HIDDEN),
        'conv_w_in': w((N_A, D, 3 * D), D),
        'conv_w': w((N_A, CONV_WIDTH, D), CONV_WIDTH),
        'conv_w_out': w((N_A, D, D), D),
        'gqa_wq': w((N_B, D, GQA_HEADS * GQA_HEAD_DIM), D),
        'gqa_wk': w((N_B, D, GQA_KV_HEADS * GQA_HEAD_DIM), D),
        'gqa_wv': w((N_B, D, GQA_KV_HEADS * GQA_HEAD_DIM), D),
        'gqa_q_norm': gain((N_B, GQA_HEAD_DIM)),
        'gqa_k_norm': gain((N_B, GQA_HEAD_DIM)),
        'gqa_wo': w((N_B, GQA_HEADS * GQA_HEAD_DIM, D), GQA_HEADS * GQA_HEAD_DIM),
        'mla_w_dq': w((N_C, D, MLA_Q_RANK), D),
        'mla_q_norm': gain((N_C, MLA_Q_RANK)),
        'mla_w_uq': w((N_C, MLA_Q_RANK, MLA_HEADS * (MLA_NOPE + MLA_ROPE)), MLA_Q_RANK),
        'mla_w_dkv': w((N_C, D, MLA_KV_RANK + MLA_ROPE), D),
        'mla_kv_norm': gain((N_C, MLA_KV_RANK)),
        'mla_w_ukv': w((N_C, MLA_KV_RANK, MLA_HEADS * (MLA_NOPE + MLA_V)), MLA_KV_RANK),
        'mla_wo': w((N_C, MLA_HEADS * MLA_V, D), MLA_HEADS * MLA_V),
        'final_g': gain((D,)),
    }


def reference(x, c, ctx, c_ctx, ada_w, ada_b, norm1_g, norm2_g, ffn_w1, ffn_w3, ffn_w2,
              conv_w_in, conv_w, conv_w_out,
              gqa_wq, gqa_wk, gqa_wv, gqa_q_norm, gqa_k_norm, gqa_wo,
              mla_w_dq, mla_q_norm, mla_w_uq, mla_w_dkv, mla_kv_norm, mla_w_ukv, mla_wo,
              final_g):
    S = x.shape[1]
    ROWS = S // GRID_W
    rows = jnp.repeat(jnp.arange(ROWS, dtype=jnp.float32), GRID_W)
    cols = jnp.tile(jnp.arange(GRID_W, dtype=jnp.float32), ROWS)
    ang_gqa = axial_angles(rows, cols, GQA_HEAD_DIM)
    ang_mla = axial_angles(rows, cols, MLA_ROPE)

    cond_lat = jax.nn.silu(c)
    cond_ctx = jax.nn.silu(c_ctx)[None]

    for i in range(DEPTH):
        kind = i % N_MIXERS
        j = i // N_MIXERS
        ctx_out = i < DEPTH - 1
        sh1, sc1, g1, sh2, sc2, g2 = modulation(cond_lat, ada_w[i], ada_b[i])
        csh1, csc1, cg1, csh2, csc2, cg2 = modulation(cond_ctx, ada_w[i], ada_b[i])

        h_lat = modulate(x, norm1_g[i], sh1, sc1)
        h_ctx = modulate(ctx, norm1_g[i], csh1, csc1) if (ctx_out or kind != 0) else None

        if kind == 0:
            y_lat = short_conv_mixer(h_lat, conv_w_in[j], conv_w[j], conv_w_out[j])
            y_ctx = short_conv_mixer(h_ctx, conv_w_in[j], conv_w[j], conv_w_out[j]) if ctx_out else None
        elif kind == 1:
            y_ctx, y_lat = gqa_mixer(h_ctx, h_lat, gqa_wq[j], gqa_wk[j], gqa_wv[j],
                                     gqa_q_norm[j], gqa_k_norm[j], gqa_wo[j], ang_gqa, ctx_out)
        else:
            y_ctx, y_lat = mla_mixer(h_ctx, h_lat, mla_w_dq[j], mla_q_norm[j], mla_w_uq[j],
                                     mla_w_dkv[j], mla_kv_norm[j], mla_w_ukv[j], mla_wo[j],
                                     ang_mla, ctx_out)

        x = x + g1 * y_lat
        x = x + g2 * swiglu(modulate(x, norm2_g[i], sh2, sc2), ffn_w1[i], ffn_w3[i], ffn_w2[i])
        if ctx_out:
            ctx = ctx + cg1 * y_ctx
            ctx = ctx + cg2 * swiglu(modulate(ctx, norm2_g[i], csh2, csc2), ffn_w1[i], ffn_w3[i], ffn_w2[i])

    return rmsnorm(x, final_g)
```

```python
import contextlib
import numpy as np
import ml_dtypes
import concourse.bass as bass
import concourse.mybir as mybir
from concourse.bass_utils import run_bass_kernel_spmd

F32 = mybir.dt.float32
BF16 = mybir.dt.bfloat16
AF = mybir.ActivationFunctionType
ALU = mybir.AluOpType

ENGS = ["pe", "act", "dve", "pool", "sp"]

D = 1024
NT = 2304
NCTX = 256
SEQ = 2048
DEPTH = 4
FH = 2816
NFC = 22
EPS = 1e-6
TCH = [(0, 256), (256, 512), (768, 512), (1280, 512), (1792, 512)]
NPL = 17
UW = 2308


class Op:
    __slots__ = ("eng", "fn", "deps", "is_dma", "need_inc", "val", "sem", "name", "seq")

    def __init__(self, eng, fn, is_dma, name):
        self.eng = eng
        self.fn = fn
        self.deps = set()
        self.is_dma = is_dma
        self.need_inc = False
        self.val = None
        self.sem = None
        self.name = name


class Em:
    def __init__(self, nc, n_dma_sems=10):
        self.nc = nc
        self.q = {e: [] for e in ENGS}
        self.lastw = {}
        self.readers = {}
        self.pending = {e: set() for e in ENGS}
        self.last_op = {e: None for e in ENGS}
        self.live_dmas = []
        self.n_dma_sems = n_dma_sems
        self.dma_rr = {e: 0 for e in ENGS}
        self.dma_last = {}
        self.dma_cnt = {}
        self.dma_engs = set()
        self.seq = 0
        self.grp_eng = None
        self.grp_ops = []
        self.grp_start = 0

    def grp_begin(self, eng="pe"):
        self.grp_eng = eng
        self.grp_ops = []
        self.grp_start = self.seq

    def grp_end(self):
        ops = self.grp_ops
        self.grp_eng = None
        self.grp_ops = []
        if len(ops) > 1:
            first = ops[0]
            for o in ops[1:]:
                mv = set(d for d in o.deps if d.seq < self.grp_start)
                first.deps |= mv
                o.deps -= mv

    def add(self, eng, fn, reads=(), writes=(), dma=False, name=""):
        op = Op(eng, fn, dma, name)
        self.seq += 1
        op.seq = self.seq
        if self.grp_eng == eng:
            self.grp_ops.append(op)
        deps = op.deps
        for k in reads:
            w = self.lastw.get(k)
            if w is not None:
                deps.add(w)
            if k[0] == "ps":
                for r in self.readers.get(k, {}).values():
                    if r.eng != eng:
                        deps.add(r)
        for k in writes:
            w = self.lastw.get(k)
            if w is not None and (dma or w.is_dma or w.eng != eng or eng != "pe"):
                deps.add(w)
            for r in self.readers.get(k, {}).values():
                if dma or r.is_dma or r.eng != eng or eng != "pe":
                    deps.add(r)
        if self.pending[eng]:
            deps |= self.pending[eng]
            self.pending[eng] = set()
        if dma:
            self.dma_engs.add(eng)
            slot = self.dma_rr[eng]
            self.dma_rr[eng] = (slot + 1) % self.n_dma_sems
            prev = self.dma_last.get((eng, slot))
            if prev is not None:
                deps.add(prev)
            self.dma_last[(eng, slot)] = op
            op.sem = (eng, slot)
            c = self.dma_cnt.get((eng, slot), 0) + 16
            self.dma_cnt[(eng, slot)] = c
            op.val = c
            self.live_dmas.append(op)
        deps.discard(op)
        for d in deps:
            d.need_inc = True
        rkey = id(op) if dma else eng
        for k in reads:
            self.readers.setdefault(k, {})[rkey] = op
        for k in writes:
            self.lastw[k] = op
            self.readers[k] = {}
        self.q[eng].append(op)
        self.last_op[eng] = op
        return op

    def barrier(self):
        s = set(o for o in self.last_op.values() if o is not None)
        s |= set(self.live_dmas)
        self.live_dmas = []
        for e in ENGS:
            self.pending[e] |= s
        self.lastw = {}
        self.readers = {}

    def emit(self, final_wait_eng="sp"):
        nc = self.nc
        self.barrier()
        self.add(final_wait_eng, lambda e: e.nop(), name="final")
        cnt = {e: 0 for e in ENGS}
        EPOCH = 4000
        for e in ENGS:
            for op in self.q[e]:
                if not op.is_dma:
                    op.sem = (e, "c", cnt[e] // EPOCH)
                    if op.need_inc:
                        op.val = cnt[e] % EPOCH + 1
                        cnt[e] += 1
        with contextlib.ExitStack() as st:
            sems = {}
            for e in ENGS:
                for ep in range(cnt[e] // EPOCH + 1):
                    sems[(e, "c", ep)] = st.enter_context(nc.semaphore("s_%s_%d" % (e, ep)))
            for e in sorted(self.dma_engs):
                for s in range(self.n_dma_sems):
                    sems[(e, s)] = st.enter_context(nc.semaphore("d_%s_%d" % (e, s)))
            block = st.enter_context(nc.Block())

            def run(eng_name):
                def body(eng):
                    seen = {}
                    for op in self.q[eng_name]:
                        need = {}
                        for d in op.deps:
                            if need.get(d.sem, 0) < d.val:
                                need[d.sem] = d.val
                        for sm in sorted(need, key=str):
                            if seen.get(sm, 0) < need[sm]:
                                eng.wait_ge(sems[sm], need[sm])
                                seen[sm] = need[sm]
                        inst = op.fn(eng)
                        if op.is_dma:
                            inst.then_inc(sems[op.sem], 16)
                        elif op.need_inc:
                            inst.then_inc(sems[op.sem], 1)
                return body

            block.tensor(run("pe"))
            block.scalar(run("act"))
            block.vector(run("dve"))
            block.gpsimd(run("pool"))
            block.sync(run("sp"))
        return cnt


class Ring:
    def __init__(self, name, n):
        self.name = name
        self.n = n
        self.i = 0

    def next(self):
        i = self.i
        self.i = (i + 1) % self.n
        return i


def _vec_layout():
    off = {}
    r = 0

    def put(name, n):
        nonlocal r
        off[name] = r
        r += n
    put("c", 8)
    put("cctx", 8)
    put("adab", DEPTH * 48)
    put("n1g", DEPTH * 8)
    put("n2g", DEPTH * 8)
    put("convw", 2 * 3 * 8)
    put("fing", 8)
    put("gqn", 1)
    put("gkn", 1)
    put("mqn", 6)
    put("mkvn", 2)
    tot = ((r + 127) // 128) * 128
    return off, tot


VOFF, VROWS = _vec_layout()


def _rope_tables():
    S = SEQ
    rows = np.repeat(np.arange(S // 64, dtype=np.float32), 64)
    cols = np.tile(np.arange(64, dtype=np.float32), S // 64)

    def ang(rot_dim):
        n = rot_dim // 4
        freqs = (np.float32(10000.0) ** (-np.arange(n, dtype=np.float32) / np.float32(n))).astype(np.float32)
        return np.concatenate([rows[:, None] * freqs, cols[:, None] * freqs], axis=-1).astype(np.float32)

    out = {}
    for nm, rd in (("g", 128), ("m", 64)):
        a = ang(rd).astype(np.float64)
        half = rd // 2
        cos = np.ones((128, UW), np.float32)
        sin = np.zeros((128, UW), np.float32)
        c = np.cos(a).T.astype(np.float32)
        s = np.sin(a).T.astype(np.float32)
        cos[0:half, NCTX:NT] = c
        cos[half:rd, NCTX:NT] = c
        sin[0:half, NCTX:NT] = s
        sin[half:rd, NCTX:NT] = s
        out[nm] = (cos, sin)
    return out


def _consts():
    ident = np.eye(128, dtype=np.float32)
    cb = np.zeros((128, 4, 128), np.float32)
    cb[:, 0, :] = 1.0
    for j in range(128):
        if j < 64:
            cb[j + 64, 1, j] = -1.0
        else:
            cb[j - 64, 1, j] = 1.0
    for j in range(64):
        if j < 32:
            cb[j + 32, 2, j] = -1.0
        else:
            cb[j - 32, 2, j] = 1.0
    cb[:, 3, :] = np.eye(128, dtype=np.float32)
    return ident, cb.astype(ml_dtypes.bfloat16)


class Builder:
    def __init__(self, layers=(0, 1, 2, 3), final_norm=True):
        self.layers = list(layers)
        self.final_norm = final_norm
        nc = bass.Bass("TRN2", target_bir_lowering=False)
        self.nc = nc
        self.E = Em(nc)
        dt = nc.dram_tensor

        def inp(name, shape, dtype=F32):
            return dt(name, list(shape), dtype, kind="ExternalInput").ap()
        self.x = inp("x", [SEQ, D])
        self.ctx = inp("ctx", [NCTX, D])
        self.vecs = inp("vecs", [VROWS, 128])
        self.ident_d = inp("ident", [128, 128])
        self.cb_d = inp("cbf", [128, 4, 128], BF16)
        self.rope_d = inp("rope", [4, 128, UW])
        self.ada_w = inp("ada_w", [DEPTH, D, 6 * D])
        self.ffn_w1 = inp("ffn_w1", [DEPTH, D, FH])
        self.ffn_w3 = inp("ffn_w3", [DEPTH, D, FH])
        self.ffn_w2 = inp("ffn_w2", [DEPTH, FH, D])
        self.conv_w_in = inp("conv_w_in", [2, D, 3 * D])
        self.conv_w_out = inp("conv_w_out", [2, D, D])
        self.gqa_wq = inp("gqa_wq", [1, D, D])
        self.gqa_wk = inp("gqa_wk", [1, D, 256])
        self.gqa_wv = inp("gqa_wv", [1, D, 256])
        self.gqa_wo = inp("gqa_wo", [1, D, D])
        self.mla_w_dq = inp("mla_w_dq", [1, D, 768])
        self.mla_w_uq = inp("mla_w_uq", [1, 768, 1536])
        self.mla_w_dkv = inp("mla_w_dkv", [1, D, 320])
        self.mla_w_ukv = inp("mla_w_ukv", [1, 256, 2048])
        self.mla_wo = inp("mla_wo", [1, D, D])
        self.out = dt("out", [SEQ, D], F32, kind="ExternalOutput").ap()
        self.build()

    def psum(self, banks=None):
        if banks is None:
            banks = range(8)
        key = tuple(banks)
        r = self._psrot.setdefault(key, [0])
        b = key[r[0] % len(key)]
        r[0] += 1
        return self.PS[b], ("ps", b)

    def tf(self):
        i = self.TFr.next()
        return self.TF[i], ("tf", i)

    def tb(self):
        i = self.TBr.next()
        return self.TB[i], ("tb", i)

    def tbp(self):
        i = self.TBPr.next()
        return self.TBP[i], ("tbp", i)

    def spawn(self, gen, lo=False, top=False):
        if top:
            self.bg_hi.insert(0, gen)
        else:
            (self.bg_lo if lo else self.bg_hi).append(gen)
        return gen

    def pump(self, k=1, banks=None, lo_ok=True):
        self.bg_banks = banks
        for _ in range(k):
            done = False
            for q in ((self.bg_hi, self.bg_lo) if lo_ok else (self.bg_hi,)):
                while q and not done:
                    try:
                        next(q[0])
                        done = True
                    except StopIteration:
                        q.pop(0)
                if done:
                    break
            if not done:
                break
        self.bg_banks = None

    def finish(self, gen, banks=None):
        while gen in self.bg_hi or gen in self.bg_lo:
            q = self.bg_hi if gen in self.bg_hi else self.bg_lo
            self.bg_banks = banks
            try:
                next(q[0])
            except StopIteration:
                q.pop(0)
        self.bg_banks = None

    def drain_hi(self, banks=None):
        while self.bg_hi:
            self.finish(self.bg_hi[0], banks)

    def load_w(self, src2d, K, W):
        i = self.Wr.next()
        slot = self.WR[i]
        kc = (K + 127) // 128
        assert kc * W <= 1024, (K, W)
        dst = slot[:, 0:kc * W].rearrange("p (c n) -> p c n", n=W)
        if K >= 128:
            srcv = src2d.rearrange("(c p) n -> p c n", p=128)
            self.E.add("pool", lambda e: e.dma_start(out=dst, in_=srcv), writes=[("w", i)], dma=True)
        else:
            srcv = src2d.rearrange("(c p) n -> p c n", p=K)
            d2 = dst[0:K]
            self.E.add("pool", lambda e: e.dma_start(out=d2, in_=srcv), writes=[("w", i)], dma=True)
        return dst, ("w", i)

    def mm(self, out, lhsT, rhs, start, stop, reads, wkey):
        self.E.add("pe", lambda e: e.matmul(out, lhsT, rhs, start=start, stop=stop), reads=reads, writes=[wkey])

    def act(self, out, in_, func, reads, writes, scale=None, bias=None):
        kw = {}
        if scale is not None:
            kw["scale"] = scale
        if bias is not None:
            kw["bias"] = bias
        self.E.add("act", lambda e: e.activation(out, in_, func, **kw), reads=reads, writes=writes)

    def tt(self, out, in0, in1, op, reads, writes, eng="dve"):
        self.E.add(eng, lambda e: e.tensor_tensor(out, in0, in1, op), reads=reads, writes=writes)

    def ts(self, out, in0, s1, s2, op0, op1, reads, writes, eng="dve"):
        if op1 is None:
            self.E.add(eng, lambda e: e.tensor_scalar(out, in0, s1, None, op0), reads=reads, writes=writes)
        else:
            self.E.add(eng, lambda e: e.tensor_scalar(out, in0, s1, s2, op0, op1), reads=reads, writes=writes)

    def stt(self, out, in0, scalar, in1, op0, op1, reads, writes):
        self.E.add("dve", lambda e: e.scalar_tensor_tensor(out, in0, scalar, in1, op0, op1), reads=reads, writes=writes)

    def pl(self, i, t0, sz):
        return self.PL[:, i, t0:t0 + sz]

    def rstd_from_ss(self, ss_ps, ssk, n_feat, sz, to_psum=False):
        t1, t1k = self.tf()
        self.act(t1[:, 0:sz], ss_ps[:, 0:sz], AF.Ln, [ssk, ("c", "eps")], [t1k], scale=1.0 / n_feat, bias=self.epsT[:, 0:1])
        if to_psum:
            t2, t2k = self.psum(self.bg_banks)
        else:
            i = self.RSr.next()
            t2, t2k = self.RS[i], ("rs", i)
        self.act(t2[:, 0:sz], t1[:, 0:sz], AF.Exp, [t1k], [t2k], scale=-0.5)
        return t2, t2k

    def phase_load(self):
        E = self.E
        nc = self.nc
        E.add("sp", lambda e: e.dma_start(out=self.ident[:], in_=self.ident_d), writes=[("c", "ident")], dma=True)
        E.add("sp", lambda e: e.dma_start(out=self.CB[:], in_=self.cb_d), writes=[("c", "cb")], dma=True)
        E.add("dve", lambda e: e.memset(self.epsT[:], EPS), writes=[("c", "eps")])
        nvt = VROWS // 128
        for j in range(nvt):
            st_, stk = self.tf()
            sv = st_[:, 0:128]
            src = self.vecs[j * 128:(j + 1) * 128, :]
            E.add("sp", lambda e, sv=sv, src=src: e.dma_start(out=sv, in_=src), writes=[stk], dma=True)
            ps, psk = self.psum()
            E.add("pe", lambda e, ps=ps, sv=sv: e.transpose(ps[:, 0:128], sv, self.ident[:]),
                  reads=[stk, ("c", "ident")], writes=[psk])
            dst = self.VT[:, j * 128:(j + 1) * 128]
            E.add("dve", lambda e, dst=dst, ps=ps: e.tensor_copy(dst, ps[:, 0:128]), reads=[psk], writes=[("vt",)])
        for j, nm in enumerate(("c", "cctx")):
            o = VOFF[nm]
            self.act(self.condT[:, :, j], self.VT[:, o:o + 8], AF.Silu, [("vt",)], [("cond",)])
        for j in range(NT // 128):
            st_, stk = self.tf()
            st2, st2k = self.tf()
            if j < 2:
                src = self.ctx[j * 128:(j + 1) * 128, :]
            else:
                src = self.x[(j - 2) * 128:(j - 1) * 128, :]
            E.add("sp", lambda e, a=st_, src=src: e.dma_start(out=a[:, 0:512], in_=src[:, 0:512]), writes=[stk], dma=True)
            E.add("sp", lambda e, a=st2, src=src: e.dma_start(out=a[:, 0:512], in_=src[:, 512:1024]), writes=[st2k], dma=True)
            n = self.chunk_of(j * 128)
            for half, (sa, sk) in enumerate(((st_, stk), (st2, st2k))):
                ps, psk = self.psum()
                for q in range(4):
                    E.add("pe", lambda e, ps=ps, sa=sa, q=q: e.transpose(ps[:, q * 128:(q + 1) * 128], sa[:, q * 128:(q + 1) * 128], self.ident[:]),
                          reads=[sk, ("c", "ident")], writes=[psk])
                dst = self.XT[:, half * 4:half * 4 + 4, j * 128:(j + 1) * 128]
                srcp = ps[:, :].rearrange("p (c t) -> p c t", t=128)
                wk = [("x", half * 4 + q, n) for q in range(4)]
                if half == 0:
                    E.add("dve", lambda e, dst=dst, srcp=srcp: e.tensor_copy(dst, srcp), reads=[psk], writes=wk)
                else:
                    E.add("act", lambda e, dst=dst, srcp=srcp: e.activation(dst, srcp, AF.Copy), reads=[psk], writes=wk)

    def chunk_of(self, t):
        for n, (t0, sz) in enumerate(TCH):
            if t0 <= t < t0 + sz:
                return n
        raise ValueError

    def gen_mod(self, l):
        ob = VOFF["adab"] + l * 48
        for fc in range(48):
            v = fc // 8
            wt, wk = self.load_w(self.ada_w[l][:, fc * 128:(fc + 1) * 128], D, 128)
            ps, psk = self.psum(self.bg_banks)
            for k in range(8):
                self.mm(ps[:, 0:2], wt[:, k, :], self.condT[:, k, :], k == 0, k == 7, [wk, ("cond",)], psk)
            self.act(self.MOD[:, l, fc, :], ps[:, 0:2], AF.Identity, [psk, ("vt",)], [("mod", l, v)],
                     bias=self.VT[:, ob + fc:ob + fc + 1])
            if fc % 8 == 7 and v in (1, 4):
                sub = 0 if v == 1 else 1
                og = VOFF["n1g" if sub == 0 else "n2g"] + l * 8
                for j in range(2):
                    sc = self.MOD[:, l, v * 8:v * 8 + 8, j]
                    A = self.MP[:, l, sub, 0, :, j]
                    self.stt(A, sc, 1.0, self.VT[:, og:og + 8], ALU.add, ALU.mult, [("mod", l, v), ("vt",)], [("mp", l, sub)])
            self.mod_done[l] = fc + 1
            yield

    def need_mod(self, l, v):
        while self.mod_done[l] < (v + 1) * 8:
            self.pump(1, None)

    def modv(self, l, s, which, c, n):
        j = 1 if n == 0 else 0
        if which == 0:
            return self.MP[:, l, s, 0, c, j:j + 1]
        v = 3 * s + (0 if which == 1 else 2)
        return self.MOD[:, l, v * 8 + c, j:j + 1]

    def modk(self, l, s, which):
        if which == 0:
            return ("mp", l, s)
        return ("mod", l, 3 * s + (0 if which == 1 else 2))

    def phase_norm(self, l, s, chunks, pump=True):
        self.need_mod(l, 3 * s + 1)
        ones = self.CB[:, 0, :]
        for n in chunks:
            if (l, s, n) in self.norm_done:
                continue
            self.norm_done.add((l, s, n))
            t0, sz = TCH[n]
            ss, ssk = self.psum()
            for c in range(8):
                sq, sqk = self.tb()
                if c % 8 not in (1, 4, 6):
                    self.act(sq[:, 0:sz], self.XT[:, c, t0:t0 + sz], AF.Square, [("x", c, n)], [sqk])
                else:
                    xa = self.XT[:, c, t0:t0 + sz]
                    self.tt(sq[:, 0:sz], xa, xa, ALU.mult, [("x", c, n)], [sqk])
                self.mm(ss[:, 0:sz], ones, sq[:, 0:sz], c == 0, c == 7, [sqk, ("c", "cb")], ssk)
            r, rk = self.rstd_from_ss(ss, ssk, D, sz, to_psum=True)
            for c in range(8):
                tmp, tmpk = self.tf()
                self.tt(tmp[:, 0:sz], self.XT[:, c, t0:t0 + sz], r[:, 0:sz], ALU.mult, [("x", c, n), rk], [tmpk])
                self.act(self.pl(c, t0, sz), tmp[:, 0:sz], AF.Identity, [tmpk, self.modk(l, s, 0), self.modk(l, s, 1)], [("pl", c, n)],
                         scale=self.modv(l, s, 0, c, n), bias=self.modv(l, s, 1, c, n))
            if pump:
                self.pump(2, None)

    def tail_split(self, chunks):
        t = self.tail
        if t is None or len(chunks) < 4:
            return [list(chunks)]
        h = 3 if len(chunks) == 5 else 2
        return [list(chunks[:h]), list(chunks[h:])]

    def tail_step(self, first_pass_chunks, d):
        t = self.tail
        if t is None:
            return
        l2, s2, cl2 = t
        todo = [n for n in first_pass_chunks if n in cl2 and (l2, s2, n) not in self.norm_done]
        if todo and d % 2 == 0:
            self.phase_norm(l2, s2, [todo[0]], pump=False)

    def resid_add(self, l, s, d, n, ps, psk):
        t0, sz = TCH[n]
        xa = self.XT[:, d, t0:t0 + sz]
        self.stt(xa, ps[:, 0:sz], self.modv(l, s, 2, d, n), xa, ALU.mult, ALU.add,
                 [psk, ("x", d, n), self.modk(l, s, 2)], [("x", d, n)])

    def phase_ffn(self, l, chunks):
        groups = [list(range(0, 8)), list(range(8, 15)), list(range(15, 22))]
        GP = 8
        for grp in groups:
            for fi, f in enumerate(grp):
                w1, w1k = self.load_w(self.ffn_w1[l][:, f * 128:(f + 1) * 128], D, 128)
                w3, w3k = self.load_w(self.ffn_w3[l][:, f * 128:(f + 1) * 128], D, 128)
                for n in chunks:
                    t0, sz = TCH[n]
                    p1, p1k = self.psum()
                    p3, p3k = self.psum()
                    for k in range(8):
                        self.mm(p1[:, 0:sz], w1[:, k, :], self.pl(k, t0, sz), k == 0, k == 7, [w1k, ("pl", k, n)], p1k)
                    for k in range(8):
                        self.mm(p3[:, 0:sz], w3[:, k, :], self.pl(k, t0, sz), k == 0, k == 7, [w3k, ("pl", k, n)], p3k)
                    s1, s1k = self.tf()
                    self.act(s1[:, 0:sz], p1[:, 0:sz], AF.Silu, [p1k], [s1k])
                    self.tt(self.pl(GP + fi, t0, sz), s1[:, 0:sz], p3[:, 0:sz], ALU.mult, [s1k, p3k], [("pl", GP + fi, n)])
                self.pump(2, None)
            ng = len(grp)
            f0 = grp[0]
            self.need_mod(l, 5)
            passes = self.tail_split(chunks) if grp is groups[-1] else [list(chunks)]
            for pi, pchunks in enumerate(passes):
                for d in range(8):
                    wts = []
                    r0 = 0
                    while r0 < ng:
                        nr = min(8, ng - r0)
                        wt, wk = self.load_w(self.ffn_w2[l][(f0 + r0) * 128:(f0 + r0 + nr) * 128, d * 128:(d + 1) * 128], nr * 128, 128)
                        for q in range(nr):
                            wts.append((wt[:, q, :], wk))
                        r0 += nr
                    for n in pchunks:
                        t0, sz = TCH[n]
                        ps, psk = self.psum()
                        for fi in range(ng):
                            self.mm(ps[:, 0:sz], wts[fi][0], self.pl(GP + fi, t0, sz), fi == 0, fi == ng - 1,
                                    [wts[fi][1], ("pl", GP + fi, n)], psk)
                        self.resid_add(l, 1, d, n, ps, psk)
                    if pi == 1:
                        self.tail_step(passes[0], d)
                    else:
                        self.pump(1, None)

    def phase_conv(self, l, chunks):
        E = self.E
        j = l // 3
        win = self.conv_w_in[j]
        BZ = 8
        ocw = VOFF["convw"] + j * 24
        UOFF = {0: 1, 1: 259, 2: 259 + 512, 3: 259 + 1024, 4: 259 + 1536}
        for u in range(2):
            for col in (0, 257, 258, 2307):
                a = self.FP[:, u, col:col + 1]
                E.add("dve", lambda e, a=a: e.memset(a, 0.0), writes=[("fpz", u)] + [("fp", u, n) for n in range(5)])
        W = {}

        def l1_load(c):
            wc, wck = self.load_w(win[:, D + c * 128:D + (c + 1) * 128], D, 128)
            wx, wxk = self.load_w(win[:, 2 * D + c * 128:2 * D + (c + 1) * 128], D, 128)
            wb, wbk = self.load_w(win[:, c * 128:(c + 1) * 128], D, 128)
            W[c] = (wb, wbk, wc, wck, wx, wxk)

        def l1(c, n):
            u = c % 2
            wb, wbk, wc, wck, wx, wxk = W[c]
            t0, sz = TCH[n]
            pc, pck = self.psum()
            px, pxk = self.psum()
            for k in range(8):
                self.mm(pc[:, 0:sz], wc[:, k, :], self.pl(k, t0, sz), k == 0, k == 7, [wck, ("pl", k, n)], pck)
            for k in range(8):
                self.mm(px[:, 0:sz], wx[:, k, :], self.pl(k, t0, sz), k == 0, k == 7, [wxk, ("pl", k, n)], pxk)
            xs, xsk = self.tf()
            self.act(xs[:, 0:sz], px[:, 0:sz], AF.Copy, [pxk], [xsk])
            o = UOFF[n]
            self.tt(self.FP[:, u, o:o + sz], pc[:, 0:sz], xs[:, 0:sz], ALU.mult, [pck, xsk], [("fp", u, n)])

        def l2(c, n):
            u = c % 2
            wb, wbk, wc, wck, wx, wxk = W[c]
            t0, sz = TCH[n]
            pb, pbk = self.psum()
            for k in range(8):
                self.mm(pb[:, 0:sz], wb[:, k, :], self.pl(k, t0, sz), k == 0, k == 7, [wbk, ("pl", k, n)], pbk)
            o = UOFF[n]
            rk = [("fp", u, n), ("fpz", u), ("vt",)]
            if n - 1 in chunks and n - 1 >= 1:
                rk.append(("fp", u, n - 1))
            if n + 1 in chunks and n >= 1:
                rk.append(("fp", u, n + 1))
            a0, a0k = self.tf()
            self.act(a0[:, 0:sz], self.FP[:, u, o - 1:o - 1 + sz], AF.Copy, rk, [a0k],
                     scale=self.VT[:, ocw + 0 + c:ocw + 0 + c + 1])
            a1, a1k = self.tf()
            self.stt(a1[:, 0:sz], self.FP[:, u, o:o + sz], self.VT[:, ocw + 8 + c:ocw + 8 + c + 1], a0[:, 0:sz],
                     ALU.mult, ALU.add, rk + [a0k], [a1k])
            a2, a2k = self.tf()
            self.stt(a2[:, 0:sz], self.FP[:, u, o + 1:o + 1 + sz], self.VT[:, ocw + 16 + c:ocw + 16 + c + 1], a1[:, 0:sz],
                     ALU.mult, ALU.add, rk + [a1k], [a2k])
            self.tt(self.pl(BZ + c, t0, sz), a2[:, 0:sz], pb[:, 0:sz], ALU.mult, [a2k, pbk], [("pl", BZ + c, n)])

        l1_load(0)
        for n in chunks:
            l1(0, n)
        for c in range(8):
            if c + 1 < 8:
                l1_load(c + 1)
            for n in chunks:
                if c + 1 < 8:
                    l1(c + 1, n)
                l2(c, n)
            self.pump(2, None)
        wout = self.conv_w_out[j]
        self.need_mod(l, 2)
        passes = self.tail_split(chunks)
        for pi, pchunks in enumerate(passes):
            for d in range(8):
                wt, wk = self.load_w(wout[:, d * 128:(d + 1) * 128], D, 128)
                for n in pchunks:
                    t0, sz = TCH[n]
                    ps, psk = self.psum()
                    for c in range(8):
                        self.mm(ps[:, 0:sz], wt[:, c, :], self.pl(BZ + c, t0, sz), c == 0, c == 7, [wk, ("pl", BZ + c, n)], psk)
                    self.resid_add(l, 0, d, n, ps, psk)
                if pi == 1:
                    self.tail_step(passes[0], d)
                else:
                    self.pump(1, None)

    def load_rope(self, which):
        E = self.E
        base = 0 if which == "g" else 2
        for u in range(2):
            src = self.rope_d[base + u]
            dst = self.FP[:, u, :]
            E.add("sp", lambda e, dst=dst, src=src: e.dma_start(out=dst, in_=src),
                  writes=[("fp", u, n) for n in range(5)] + [("fpz", u)], dma=True)

    def rope_apply(self, qn, qnk, np_, Rm, n, dst, dstk):
        t0, sz = TCH[n]
        rp, rpk = self.psum(self.bg_banks)
        self.mm(rp[0:np_, 0:sz], Rm, qn, True, True, [qnk, ("c", "cb")], rpk)
        cos = self.FP[0:np_, 0, t0:t0 + sz]
        sin = self.FP[0:np_, 1, t0:t0 + sz]
        a, ak = self.tf()
        self.tt(a[0:np_, 0:sz], qn, cos, ALU.mult, [qnk, ("fp", 0, n)], [ak])
        b, bk = self.tf()
        self.tt(b[0:np_, 0:sz], rp[0:np_, 0:sz], sin, ALU.mult, [rpk, ("fp", 1, n)], [bk])
        self.tt(dst, a[0:np_, 0:sz], b[0:np_, 0:sz], ALU.add, [ak, bk], [dstk])

    SB = (0, 1, 2, 3)
    OB = (4,)
    SUMB = (5,)
    PB = (6, 7)

    def attn_core(self, qchunks, kparts, qparts, vplane, oplane, scale):
        ones = self.CB[:, 0, :]
        E = self.E
        G, LOOK = 2, 4
        for n in qchunks:
            t0, sz = TCH[n]
            tiles = list(range(2)) if n == 0 else list(range(18))
            accO, accOk = self.psum(self.OB)
            accS, accSk = self.psum(self.SUMB)
            pend = []
            nt = len(tiles)
            st = {"done": 0}

            def issue_s(jt):
                s, sk = self.psum(self.SB)
                tn = self.chunk_of(jt * 128)
                for pi, ((kp, npk), (qp, npq)) in enumerate(zip(kparts, qparts)):
                    self.mm(s[:, 0:sz], self.PL[0:npk, kp, jt * 128:(jt + 1) * 128], self.PL[0:npq, qp, t0:t0 + sz],
                            pi == 0, pi == len(kparts) - 1, [("pl", kp, tn), ("pl", qp, n)], sk)
                p, pk = self.tbp()
                self.act(p[:, 0:sz], s[:, 0:sz], AF.Exp, [sk], [pk], scale=scale)
                pend.append((jt, p, pk))

            def issue_pv():
                jt, p, pk = pend.pop(0)
                first = st["done"] == 0
                last = st["done"] == nt - 1
                st["done"] += 1
                tn = self.chunk_of(jt * 128)
                self.mm(accO[:, 0:sz], self.PL[:, vplane, jt * 128:(jt + 1) * 128], p[:, 0:sz], first, last,
                        [("pl", vplane, tn), pk], accOk)
                self.mm(accS[:, 0:sz], ones, p[:, 0:sz], first, last, [pk, ("c", "cb")], accSk)

            for g0 in range(0, nt, G):
                E.grp_begin("pe")
                for jt in tiles[g0:g0 + G]:
                    issue_s(jt)
                while len(pend) > LOOK:
                    issue_pv()
                E.grp_end()
                self.pump(1, self.PB)
            while pend:
                E.grp_begin("pe")
                for _ in range(min(G, len(pend))):
                    issue_pv()
                E.grp_end()
                self.pump(1, self.PB)
            op_ = self.pl(oplane, t0, sz)
            rc, rck = self.RC, ("rcb",)
            if self.norm_gen is not None:
                self.finish(self.norm_gen, self.PB)
            self.E.add("dve", lambda e, op_=op_, accO=accO, sz=sz: e.tensor_copy(op_, accO[:, 0:sz]), reads=[accOk], writes=[("pl", oplane, n)])
            self.act(rc[:, 0:sz], accS[:, 0:sz], AF.Ln, [accSk], [rck])
            self.norm_gen = self.spawn(self.gen_normalise(op_, ("pl", oplane, n), sz), top=True)

    def gen_normalise(self, op_, opk, sz):
        rc, rck = self.RC, ("rcb",)
        self.act(rc[:, 0:sz], rc[:, 0:sz], AF.Exp, [rck], [rck], scale=-1.0)
        yield
        self.tt(op_, op_, rc[:, 0:sz], ALU.mult, [opk, rck], [opk])
        yield

    def gen_out_proj(self, l, wo, heads_planes, chunks, last=False):
        nh = len(heads_planes)
        h0 = heads_planes[0][0]
        passes = self.tail_split(chunks) if last else [list(chunks)]
        for pi, pchunks in enumerate(passes):
            for d in range(8):
                wt, wk = self.load_w(wo[h0 * 128:(h0 + nh) * 128, d * 128:(d + 1) * 128], nh * 128, 128)
                for n in pchunks:
                    t0, sz = TCH[n]
                    ps, psk = self.psum(self.bg_banks)
                    for i, (h, plane) in enumerate(heads_planes):
                        self.mm(ps[:, 0:sz], wt[:, i, :], self.pl(plane, t0, sz), i == 0, i == nh - 1, [wk, ("pl", plane, n)], psk)
                    self.resid_add(l, 0, d, n, ps, psk)
                    yield
                if last and pi == 1:
                    self.tail_step(passes[0], d)

    def gen_pnr(self, wsrc, gcol, dplane):
        ones = self.CB[:, 0, :]
        R128 = self.CB[:, 1, :]
        wt, wk = self.load_w(wsrc, D, 128)
        stt_ = {}

        def s1(n):
            t0, sz = TCH[n]
            ps, psk = self.psum(self.bg_banks)
            for k in range(8):
                self.mm(ps[:, 0:sz], wt[:, k, :], self.pl(k, t0, sz), k == 0, k == 7, [wk, ("pl", k, n)], psk)
            sq, sqk = self.tb()
            qr, qrk = self.tb()
            self.E.add("dve", lambda e: e.tensor_copy(qr[:, 0:sz], ps[:, 0:sz]), reads=[psk], writes=[qrk])
            self.tt(sq[:, 0:sz], qr[:, 0:sz], qr[:, 0:sz], ALU.mult, [qrk], [sqk])
            stt_[n] = (sq, sqk, qr, qrk)

        def s2(n):
            t0, sz = TCH[n]
            sq, sqk, qr, qrk = stt_[n]
            ss, ssk = self.psum(self.bg_banks)
            self.mm(ss[:, 0:sz], ones, sq[:, 0:sz], True, True, [sqk, ("c", "cb")], ssk)
            r, rk = self.rstd_from_ss(ss, ssk, 128, sz)
            self.stt(qr[:, 0:sz], qr[:, 0:sz], self.VT[:, gcol:gcol + 1], r[:, 0:sz], ALU.mult, ALU.mult,
                     [qrk, rk, ("vt",)], [qrk])

        def s3(n):
            t0, sz = TCH[n]
            sq, sqk, qr, qrk = stt_[n]
            self.rope_apply(qr[:, 0:sz], qrk, 128, R128, n, self.pl(dplane, t0, sz), ("pl", dplane, n))

        for pair in ((0, 1), (2, 3), (4,)):
            for st in (s1, s2, s3):
                for n in pair:
                    st(n)
                    yield
                if len(pair) == 1:
                    yield

    def run_rr(self, gens):
        gens = list(gens)
        while gens:
            for g in list(gens):
                try:
                    next(g)
                except StopIteration:
                    gens.remove(g)

    def phase_gqa(self, l, chunks):
        self.need_mod(l, 2)
        self.load_rope("g")
        KP = (8, 9)
        VP = (10, 11)
        QP = (12, 13, 14, 15)
        scale = 128.0 ** -0.5
        self.bg_banks = None
        def gen_v(g):
            wt, wk = self.load_w(self.gqa_wv[0][:, g * 128:(g + 1) * 128], D, 128)
            for jt in range(18):
                n = self.chunk_of(jt * 128)
                ps, psk = self.psum()
                for k in range(8):
                    self.mm(ps[:, 0:128], self.PL[:, k, jt * 128:(jt + 1) * 128], wt[:, k, :], k == 0, k == 7, [wk, ("pl", k, n)], psk)
                self.E.add("dve", lambda e, a=self.PL[:, VP[g], jt * 128:(jt + 1) * 128], ps=ps: e.tensor_copy(a, ps[:, 0:128]),
                           reads=[psk], writes=[("pl", VP[g], n)])
                yield

        for g in range(2):
            self.run_rr([self.gen_pnr(self.gqa_wk[0][:, g * 128:(g + 1) * 128], VOFF["gkn"], KP[g]), gen_v(g)])
        gq = self.spawn(self.gen_pnr(self.gqa_wq[0][:, 0:128], VOFF["gqn"], QP[0]))
        self.finish(gq, None)
        for h in range(8):
            g = h // 4
            qp = QP[h % 4]
            gq = None
            if h + 1 < 8:
                gq = self.spawn(self.gen_pnr(self.gqa_wq[0][:, (h + 1) * 128:(h + 2) * 128], VOFF["gqn"], QP[(h + 1) % 4]))
            self.attn_core(chunks, [(KP[g], 128)], [(qp, 128)], VP[g], qp, scale)
            if h % 2 == 1:
                self.spawn(self.gen_out_proj(l, self.gqa_wo[0], [(h - 1, QP[(h - 1) % 4]), (h, qp)], chunks, last=(h == 7)))
            if gq is not None:
                self.finish(gq, None)
        self.drain_hi(None)

    def gen_mla_head(self, h, chunks, CQ, CKV):
        par = h % 2
        Kp, Vp, Qn, Qpe = 0 + 4 * par, 1 + 4 * par, 2 + 4 * par, 3 + 4 * par
        R64 = self.CB[0:64, 2, 0:64]
        wuq = self.mla_w_uq[0]
        wukv = self.mla_w_ukv[0]
        wt, wk = self.load_w(wukv[:, h * 256:h * 256 + 128], 256, 128)
        for n in range(5):
            t0, sz = TCH[n]
            ps, psk = self.psum(self.bg_banks)
            for k in range(2):
                self.mm(ps[:, 0:sz], wt[:, k, :], self.pl(CKV[k], t0, sz), k == 0, k == 1, [wk, ("pl", CKV[k], n)], psk)
            self.act(self.pl(Kp, t0, sz), ps[:, 0:sz], AF.Copy, [psk], [("pl", Kp, n)])
            yield
        wt, wk = self.load_w(wukv[:, h * 256 + 128:h * 256 + 256], 256, 128)
        for j0 in range(0, 18, 4):
            nj = min(4, 18 - j0)
            n = self.chunk_of(j0 * 128)
            assert self.chunk_of((j0 + nj - 1) * 128) == n or True
            ps, psk = self.psum(self.bg_banks)
            rks = set()
            for q in range(nj):
                jt = j0 + q
                nn = self.chunk_of(jt * 128)
                rks.add(nn)
                for k in range(2):
                    self.mm(ps[:, q * 128:(q + 1) * 128], self.PL[:, CKV[k], jt * 128:(jt + 1) * 128], wt[:, k, :], k == 0, k == 1,
                            [wk, ("pl", CKV[k], nn)], psk)
            a = self.PL[:, Vp, j0 * 128:(j0 + nj) * 128]
            self.E.add("dve", lambda e, a=a, ps=ps, nj=nj: e.tensor_copy(a, ps[:, 0:nj * 128]),
                       reads=[psk], writes=[("pl", Vp, nn) for nn in sorted(rks)])
            yield
        wt, wk = self.load_w(wuq[:, h * 192:h * 192 + 128], 768, 128)
        for n in chunks:
            t0, sz = TCH[n]
            ps, psk = self.psum(self.bg_banks)
            for k in range(6):
                self.mm(ps[:, 0:sz], wt[:, k, :], self.pl(CQ[k], t0, sz), k == 0, k == 5, [wk, ("pl", CQ[k], n)], psk)
            self.act(self.pl(Qn, t0, sz), ps[:, 0:sz], AF.Copy, [psk], [("pl", Qn, n)])
            yield
        wt, wk = self.load_w(wuq[:, h * 192 + 128:h * 192 + 192], 768, 64)
        qs = {}

        def sa(n):
            t0, sz = TCH[n]
            ps, psk = self.psum(self.bg_banks)
            for k in range(6):
                self.mm(ps[0:64, 0:sz], wt[:, k, :], self.pl(CQ[k], t0, sz), k == 0, k == 5, [wk, ("pl", CQ[k], n)], psk)
            qn, qnk = self.tb()
            self.act(qn[0:64, 0:sz], ps[0:64, 0:sz], AF.Copy, [psk], [qnk])
            qs[n] = (qn, qnk)

        def sb_(n):
            t0, sz = TCH[n]
            qn, qnk = qs[n]
            self.rope_apply(qn[0:64, 0:sz], qnk, 64, R64, n, self.PL[0:64, Qpe, t0:t0 + sz], ("pl", Qpe, n))

        cl = list(chunks)
        for i in range(len(cl) + 2):
            if i < len(cl):
                sa(cl[i])
                yield
            if i >= 2:
                sb_(cl[i - 2])
                yield

    def phase_mla(self, l, chunks):
        ones = self.CB[:, 0, :]
        R64 = self.CB[0:64, 2, 0:64]
        self.need_mod(l, 2)
        self.load_rope("m")
        CQ = list(range(8, 14))
        CKV = (14, 15)
        KPE = 16
        scale = 192.0 ** -0.5
        wdq = self.mla_w_dq[0]
        wdkv = self.mla_w_dkv[0]
        self.bg_banks = None

        def latent(wsrc, ncol_chunks, planes, gbase):
            prev = [None]
            accs = [self.psum((3, 4, 5, 6, 7)) for _ in range(5)]
            for c in range(ncol_chunks):
                wt, wk = self.load_w(wsrc[:, c * 128:(c + 1) * 128], D, 128)
                for n in range(5):
                    t0, sz = TCH[n]
                    ps, psk = self.psum((0, 1, 2))
                    for k in range(8):
                        self.mm(ps[:, 0:sz], wt[:, k, :], self.pl(k, t0, sz), k == 0, k == 7, [wk, ("pl", k, n)], psk)
                    sq, sqk = self.tb()
                    self.act(sq[:, 0:sz], ps[:, 0:sz], AF.Square, [psk], [sqk])
                    self.act(self.pl(planes[c], t0, sz), ps[:, 0:sz], AF.Copy, [psk, ("vt",)], [("pl", planes[c], n)],
                             scale=self.VT[:, gbase + c:gbase + c + 1])
                    if prev[0] is not None:
                        prev[0]()
                    a, ak = accs[n]

                    def _ones(a=a, ak=ak, sq=sq, sqk=sqk, sz=sz, c=c):
                        self.mm(a[:, 0:sz], ones, sq[:, 0:sz], c == 0, c == ncol_chunks - 1, [sqk, ("c", "cb")], ak)
                    prev[0] = _ones
            prev[0]()
            prev[0] = None
            for n in range(5):
                t0, sz = TCH[n]
                a, ak = accs[n]
                r, rk = self.rstd_from_ss(a, ak, ncol_chunks * 128, sz)
                for c in range(ncol_chunks):
                    pa = self.pl(planes[c], t0, sz)
                    self.tt(pa, pa, r[:, 0:sz], ALU.mult, [("pl", planes[c], n), rk], [("pl", planes[c], n)])

        latent(wdq, 6, CQ, VOFF["mqn"])
        latent(wdkv, 2, CKV, VOFF["mkvn"])
        wt, wk = self.load_w(wdkv[:, 256:320], D, 64)
        for n in range(5):
            t0, sz = TCH[n]
            ps, psk = self.psum()
            for k in range(8):
                self.mm(ps[0:64, 0:sz], wt[:, k, :], self.pl(k, t0, sz), k == 0, k == 7, [wk, ("pl", k, n)], psk)
            kn, knk = self.tb()
            self.act(kn[0:64, 0:sz], ps[0:64, 0:sz], AF.Copy, [psk], [knk])
            self.rope_apply(kn[0:64, 0:sz], knk, 64, R64, n, self.PL[0:64, KPE, t0:t0 + sz], ("pl", KPE, n))
        for plane in (KPE, 3, 7):
            a = self.PL[64:128, plane, :]
            self.E.add("dve", lambda e, a=a: e.memset(a, 0.0), writes=[("pl", plane, n) for n in range(5)])
        gh = self.spawn(self.gen_mla_head(0, chunks, CQ, CKV))
        self.finish(gh, None)
        for h in range(8):
            par = h % 2
            Kp, Vp, Qn, Qpe = 0 + 4 * par, 1 + 4 * par, 2 + 4 * par, 3 + 4 * par
            gh = None
            if h + 1 < 8:
                gh = self.spawn(self.gen_mla_head(h + 1, chunks, CQ, CKV))
            self.attn_core(chunks, [(Kp, 128), (KPE, 128)], [(Qn, 128), (Qpe, 128)], Vp, Qn, scale)
            if gh is not None:
                self.finish(gh, None)
            self.spawn(self.gen_out_proj(l, self.mla_wo[0], [(h, Qn)], chunks, last=(h == 7)))
        self.drain_hi(None)

    def phase_final(self):
        E = self.E
        ones = self.CB[:, 0, :]
        og = VOFF["fing"]
        for n in range(1, 5):
            t0, sz = TCH[n]
            if self.final_norm:
                ss, ssk = self.psum()
                for c in range(8):
                    sq, sqk = self.tb()
                    self.act(sq[:, 0:sz], self.XT[:, c, t0:t0 + sz], AF.Square, [("x", c, n)], [sqk])
                    self.mm(ss[:, 0:sz], ones, sq[:, 0:sz], c == 0, c == 7, [sqk, ("c", "cb")], ssk)
                r, rk = self.rstd_from_ss(ss, ssk, D, sz)
            ys = []
            for c in range(8):
                if self.final_norm:
                    y, yk = self.FP[:, 0, (c % 4) * 512:(c % 4) * 512 + 512], ("fy", c % 4)
                    self.stt(y[:, 0:sz], self.XT[:, c, t0:t0 + sz], self.VT[:, og + c:og + c + 1], r[:, 0:sz], ALU.mult, ALU.mult,
                             [("x", c, n), rk, ("vt",)], [yk])
                    ys.append((y, yk, 0))
                else:
                    ys.append((self.XT[:, c, :], ("x", c, n), t0))
                if c % 4 == 3:
                    for tt_ in range(sz // 128):
                        ps, psk = self.psum()
                        for q in range(4):
                            y, yk, yo = ys[q]
                            E.add("pe", lambda e, ps=ps, y=y, q=q, tt_=tt_, yo=yo: e.transpose(
                                ps[:, q * 128:(q + 1) * 128], y[:, yo + tt_ * 128:yo + (tt_ + 1) * 128], self.ident[:]),
                                reads=[yk, ("c", "ident")], writes=[psk])
                        ot, otk = self.tf()
                        if tt_ % 2 == 0:
                            E.add("act", lambda e, ot=ot, ps=ps: e.activation(ot[:, :], ps[:, :], AF.Copy), reads=[psk], writes=[otk])
                        else:
                            E.add("dve", lambda e, ot=ot, ps=ps: e.tensor_copy(ot[:, :], ps[:, :]), reads=[psk], writes=[otk])
                        r0 = t0 - NCTX + tt_ * 128
                        cb = (c // 4) * 512
                        dst = self.out[r0:r0 + 128, cb:cb + 512]
                        E.add("sp", lambda e, dst=dst, ot=ot: e.dma_start(out=dst, in_=ot[:, :]), reads=[otk], dma=True)
                    ys = []

    def build(self):
        nc = self.nc
        self._psrot = {}
        self.bg_hi = []
        self.bg_lo = []
        self.bg_banks = None
        self.norm_gen = None
        self.tail = None
        self.norm_done = set()
        self.mod_done = {l: 0 for l in range(DEPTH)}
        with contextlib.ExitStack() as st:
            sb = lambda name, shape, dtype: st.enter_context(nc.sbuf_tensor(name, shape, dtype))
            self.XT = sb("XT", [128, 8, NT], F32)
            self.PL = sb("PL", [128, NPL, NT], BF16)
            self.FP = sb("FP", [128, 2, UW], F32)
            NW = 7
            self.WR = [sb("WR%d" % i, [128, 1024], BF16) for i in range(NW)]
            self.Wr = Ring("w", NW)
            self.TF = [sb("TF%d" % i, [128, 512], F32) for i in range(3)]
            self.TFr = Ring("tf", 3)
            self.RC = sb("RC", [128, 512], F32)
            self.TB = [sb("TB%d" % i, [128, 512], BF16) for i in range(4)]
            self.TBr = Ring("tb", 4)
            self.TBP = [sb("TBP%d" % i, [128, 512], BF16) for i in range(6)]
            self.TBPr = Ring("tbp", 6)
            self.RS = [sb("RS%d" % i, [128, 512], F32) for i in range(2)]
            self.RSr = Ring("rs", 2)
            self.ident = sb("ident_sb", [128, 128], F32)
            self.CB = sb("CB", [128, 4, 128], BF16)
            self.VT = sb("VT", [128, VROWS], F32)
            self.condT = sb("condT", [128, 8, 2], BF16)
            self.MOD = sb("MOD", [128, DEPTH, 48, 2], F32)
            self.MP = sb("MP", [128, DEPTH, 2, 1, 8, 2], F32)
            self.epsT = sb("epsT", [128, 1], F32)
            self.PS = [st.enter_context(nc.psum_tensor("PS%d" % i, [128, 512], F32)) for i in range(8)]

            self.phase_load()
            for li, l in enumerate(self.layers):
                if li == 0:
                    self.spawn(self.gen_mod(l), lo=True)
                if li + 1 < len(self.layers):
                    self.spawn(self.gen_mod(self.layers[li + 1]), lo=True)
                kind = l % 3
                last = (l == DEPTH - 1)
                chunks = [1, 2, 3, 4] if last else [0, 1, 2, 3, 4]
                nchunks = chunks if kind == 0 else [0, 1, 2, 3, 4]
                self.phase_norm(l, 0, nchunks)
                self.tail = (l, 1, chunks)
                if kind == 0:
                    self.phase_conv(l, chunks)
                elif kind == 1:
                    self.phase_gqa(l, chunks)
                else:
                    self.phase_mla(l, chunks)
                self.tail = None
                self.phase_norm(l, 1, chunks)
                if li + 1 < len(self.layers):
                    l2 = self.layers[li + 1]
                    last2 = (l2 == DEPTH - 1)
                    ch2 = [1, 2, 3, 4] if last2 else [0, 1, 2, 3, 4]
                    self.tail = (l2, 0, ch2 if l2 % 3 == 0 else [0, 1, 2, 3, 4])
                self.phase_ffn(l, chunks)
                self.tail = None
            while self.bg_lo or self.bg_hi:
                self.pump(1, None)
            self.E.barrier()
            self.phase_final()
            self.sem_counts = self.E.emit()


_CACHE = {}


def _host_consts():
    if "c" not in _CACHE:
        ident, cb = _consts()
        rt = _rope_tables()
        rope = np.stack([rt["g"][0], rt["g"][1], rt["m"][0], rt["m"][1]]).astype(np.float32)
        _CACHE["c"] = (ident, cb, rope)
    return _CACHE["c"]


def make_in_maps(inputs, cores):
    ident, cb, rope = _host_consts()
    f = lambda a: np.ascontiguousarray(np.asarray(a, dtype=np.float32))
    shared = {}
    for k in ("ada_w", "ffn_w1", "ffn_w3", "ffn_w2", "conv_w_in", "conv_w_out", "gqa_wq", "gqa_wk", "gqa_wv",
              "gqa_wo", "mla_w_dq", "mla_w_uq", "mla_w_dkv", "mla_w_ukv", "mla_wo"):
        shared[k] = f(inputs[k])
    shared["ident"] = ident
    shared["cbf"] = cb
    shared["rope"] = rope
    x = f(inputs["x"])
    c = f(inputs["c"])
    ctx = f(inputs["ctx"])
    maps = []
    for b in cores:
        vecs = np.zeros((VROWS, 128), np.float32)

        def put(name, arr):
            a = f(arr).reshape(-1, 128)
            vecs[VOFF[name]:VOFF[name] + a.shape[0]] = a
        put("c", c[b])
        put("cctx", inputs["c_ctx"])
        put("adab", inputs["ada_b"])
        put("n1g", inputs["norm1_g"])
        put("n2g", inputs["norm2_g"])
        put("convw", inputs["conv_w"])
        put("fing", inputs["final_g"])
        put("gqn", inputs["gqa_q_norm"])
        put("gkn", inputs["gqa_k_norm"])
        put("mqn", inputs["mla_q_norm"])
        put("mkvn", inputs["mla_kv_norm"])
        m = dict(shared)
        m["x"] = x[b]
        m["ctx"] = ctx[b]
        m["vecs"] = vecs
        maps.append(m)
    return maps


def kernel(**inputs):
    if "nc" not in _CACHE:
        _CACHE["nc"] = Builder().nc
    nc = _CACHE["nc"]
    maps = make_in_maps(inputs, list(range(8)))
    res = run_bass_kernel_spmd(nc, maps, core_ids=list(range(8)))
    return np.stack([r["out"] for r in res.results], axis=0).astype(np.float32)
```

```python
import contextlib
import numpy as np
import ml_dtypes
import concourse.bass as bass
import concourse.mybir as mybir
from concourse.bass_utils import run_bass_kernel_spmd

F32 = mybir.dt.float32
BF16 = mybir.dt.bfloat16
AF = mybir.ActivationFunctionType
ALU = mybir.AluOpType

ENGS = ["pe", "act", "dve", "pool", "sp"]

D = 1024
NT = 2304
NCTX = 256
SEQ = 2048
DEPTH = 4
FH = 2816
NFC = 22
EPS = 1e-6
TCH = [(0, 256), (256, 512), (768, 512), (1280, 512), (1792, 512)]
NPL = 17
UW = 2308


class Op:
    __slots__ = ("eng", "fn", "deps", "is_dma", "need_inc", "val", "sem", "name", "seq")

    def __init__(self, eng, fn, is_dma, name):
        self.eng = eng
        self.fn = fn
        self.deps = set()
        self.is_dma = is_dma
        self.need_inc = False
        self.val = None
        self.sem = None
        self.name = name


class Em:
    def __init__(self, nc, n_dma_sems=10):
        self.nc = nc
        self.q = {e: [] for e in ENGS}
        self.lastw = {}
        self.readers = {}
        self.pending = {e: set() for e in ENGS}
        self.last_op = {e: None for e in ENGS}
        self.live_dmas = []
        self.n_dma_sems = n_dma_sems
        self.dma_rr = {e: 0 for e in ENGS}
        self.dma_last = {}
        self.dma_cnt = {}
        self.dma_engs = set()
        self.seq = 0
        self.grp_eng = None
        self.grp_ops = []
        self.grp_start = 0

    def grp_begin(self, eng="pe"):
        self.grp_eng = eng
        self.grp_ops = []
        self.grp_start = self.seq

    def grp_end(self):
        ops = self.grp_ops
        self.grp_eng = None
        self.grp_ops = []
        if len(ops) > 1:
            first = ops[0]
            for o in ops[1:]:
                mv = set(d for d in o.deps if d.seq < self.grp_start)
                first.deps |= mv
                o.deps -= mv

    def add(self, eng, fn, reads=(), writes=(), dma=False, name=""):
        op = Op(eng, fn, dma, name)
        self.seq += 1
        op.seq = self.seq
        if self.grp_eng == eng:
            self.grp_ops.append(op)
        deps = op.deps
        for k in reads:
            w = self.lastw.get(k)
            if w is not None:
                deps.add(w)
            if k[0] == "ps":
                for r in self.readers.get(k, {}).values():
                    if r.eng != eng:
                        deps.add(r)
        for k in writes:
            w = self.lastw.get(k)
            if w is not None and (dma or w.is_dma or w.eng != eng or eng != "pe"):
                deps.add(w)
            for r in self.readers.get(k, {}).values():
                if dma or r.is_dma or r.eng != eng or eng != "pe":
                    deps.add(r)
        if self.pending[eng]:
            deps |= self.pending[eng]
            self.pending[eng] = set()
        if dma:
            self.dma_engs.add(eng)
            slot = self.dma_rr[eng]
            self.dma_rr[eng] = (slot + 1) % self.n_dma_sems
            prev = self.dma_last.get((eng, slot))
            if prev is not None:
                deps.add(prev)
            self.dma_last[(eng, slot)] = op
            op.sem = (eng, slot)
            c = self.dma_cnt.get((eng, slot), 0) + 16
            self.dma_cnt[(eng, slot)] = c
            op.val = c
            self.live_dmas.append(op)
        deps.discard(op)
        for d in deps:
            d.need_inc = True
        rkey = id(op) if dma else eng
        for k in reads:
            self.readers.setdefault(k, {})[rkey] = op
        for k in writes:
            self.lastw[k] = op
            self.readers[k] = {}
        self.q[eng].append(op)
        self.last_op[eng] = op
        return op

    def barrier(self):
        s = set(o for o in self.last_op.values() if o is not None)
        s |= set(self.live_dmas)
        self.live_dmas = []
        for e in ENGS:
            self.pending[e] |= s
        self.lastw = {}
        self.readers = {}

    def emit(self, final_wait_eng="sp"):
        nc = self.nc
        self.barrier()
        self.add(final_wait_eng, lambda e: e.nop(), name="final")
        cnt = {e: 0 for e in ENGS}
        EPOCH = 4000
        for e in ENGS:
            for op in self.q[e]:
                if not op.is_dma:
                    op.sem = (e, "c", cnt[e] // EPOCH)
                    if op.need_inc:
                        op.val = cnt[e] % EPOCH + 1
                        cnt[e] += 1
        with contextlib.ExitStack() as st:
            sems = {}
            for e in ENGS:
                for ep in range(cnt[e] // EPOCH + 1):
                    sems[(e, "c", ep)] = st.enter_context(nc.semaphore("s_%s_%d" % (e, ep)))
            for e in sorted(self.dma_engs):
                for s in range(self.n_dma_sems):
                    sems[(e, s)] = st.enter_context(nc.semaphore("d_%s_%d" % (e, s)))
            block = st.enter_context(nc.Block())

            def run(eng_name):
                def body(eng):
                    seen = {}
                    for op in self.q[eng_name]:
                        need = {}
                        for d in op.deps:
                            if need.get(d.sem, 0) < d.val:
                                need[d.sem] = d.val
                        for sm in sorted(need, key=str):
                            if seen.get(sm, 0) < need[sm]:
                                eng.wait_ge(sems[sm], need[sm])
                                seen[sm] = need[sm]
                        inst = op.fn(eng)
                        if op.is_dma:
                            inst.then_inc(sems[op.sem], 16)
                        elif op.need_inc:
                            inst.then_inc(sems[op.sem], 1)
                return body

            block.tensor(run("pe"))
            block.scalar(run("act"))
            block.vector(run("dve"))
            block.gpsimd(run("pool"))
            block.sync(run("sp"))
        return cnt


class Ring:
    def __init__(self, name, n):
        self.name = name
        self.n = n
        self.i = 0

    def next(self):
        i = self.i
        self.i = (i + 1) % self.n
        return i


def _vec_layout():
    off = {}
    r = 0

    def put(name, n):
        nonlocal r
        off[name] = r
        r += n
    put("c", 8)
    put("cctx", 8)
    put("adab", DEPTH * 48)
    put("n1g", DEPTH * 8)
    put("n2g", DEPTH * 8)
    put("convw", 2 * 3 * 8)
    put("fing", 8)
    put("gqn", 1)
    put("gkn", 1)
    put("mqn", 6)
    put("mkvn", 2)
    tot = ((r + 127) // 128) * 128
    return off, tot


VOFF, VROWS = _vec_layout()


def _rope_tables():
    S = SEQ
    rows = np.repeat(np.arange(S // 64, dtype=np.float32), 64)
    cols = np.tile(np.arange(64, dtype=np.float32), S // 64)

    def ang(rot_dim):
        n = rot_dim // 4
        freqs = (np.float32(10000.0) ** (-np.arange(n, dtype=np.float32) / np.float32(n))).astype(np.float32)
        return np.concatenate([rows[:, None] * freqs, cols[:, None] * freqs], axis=-1).astype(np.float32)

    out = {}
    for nm, rd in (("g", 128), ("m", 64)):
        a = ang(rd).astype(np.float64)
        half = rd // 2
        cos = np.ones((128, UW), np.float32)
        sin = np.zeros((128, UW), np.float32)
        c = np.cos(a).T.astype(np.float32)
        s = np.sin(a).T.astype(np.float32)
        cos[0:half, NCTX:NT] = c
        cos[half:rd, NCTX:NT] = c
        sin[0:half, NCTX:NT] = s
        sin[half:rd, NCTX:NT] = s
        out[nm] = (cos, sin)
    return out


def _consts():
    ident = np.eye(128, dtype=np.float32)
    cb = np.zeros((128, 4, 128), np.float32)
    cb[:, 0, :] = 1.0
    for j in range(128):
        if j < 64:
            cb[j + 64, 1, j] = -1.0
        else:
            cb[j - 64, 1, j] = 1.0
    for j in range(64):
        if j < 32:
            cb[j + 32, 2, j] = -1.0
        else:
            cb[j - 32, 2, j] = 1.0
    cb[:, 3, :] = np.eye(128, dtype=np.float32)
    return ident, cb.astype(ml_dtypes.bfloat16)


class Builder:
    def __init__(self, layers=(0, 1, 2, 3), final_norm=True):
        self.layers = list(layers)
        self.final_norm = final_norm
        nc = bass.Bass("TRN2", target_bir_lowering=False)
        self.nc = nc
        self.E = Em(nc)
        dt = nc.dram_tensor

        def inp(name, shape, dtype=F32):
            return dt(name, list(shape), dtype, kind="ExternalInput").ap()
        self.x = inp("x", [SEQ, D])
        self.ctx = inp("ctx", [NCTX, D])
        self.vecs = inp("vecs", [VROWS, 128])
        self.ident_d = inp("ident", [128, 128])
        self.cb_d = inp("cbf", [128, 4, 128], BF16)
        self.rope_d = inp("rope", [4, 128, UW])
        self.ada_w = inp("ada_w", [DEPTH, D, 6 * D])
        self.ffn_w1 = inp("ffn_w1", [DEPTH, D, FH])
        self.ffn_w3 = inp("ffn_w3", [DEPTH, D, FH])
        self.ffn_w2 = inp("ffn_w2", [DEPTH, FH, D])
        self.conv_w_in = inp("conv_w_in", [2, D, 3 * D])
        self.conv_w_out = inp("conv_w_out", [2, D, D])
        self.gqa_wq = inp("gqa_wq", [1, D, D])
        self.gqa_wk = inp("gqa_wk", [1, D, 256])
        self.gqa_wv = inp("gqa_wv", [1, D, 256])
        self.gqa_wo = inp("gqa_wo", [1, D, D])
        self.mla_w_dq = inp("mla_w_dq", [1, D, 768])
        self.mla_w_uq = inp("mla_w_uq", [1, 768, 1536])
        self.mla_w_dkv = inp("mla_w_dkv", [1, D, 320])
        self.mla_w_ukv = inp("mla_w_ukv", [1, 256, 2048])
        self.mla_wo = inp("mla_wo", [1, D, D])
        self.out = dt("out", [SEQ, D], F32, kind="ExternalOutput").ap()
        self.build()

    def psum(self, banks=None):
        if banks is None:
            banks = range(8)
        key = tuple(banks)
        r = self._psrot.setdefault(key, [0])
        b = key[r[0] % len(key)]
        r[0] += 1
        return self.PS[b], ("ps", b)

    def tf(self):
        i = self.TFr.next()
        return self.TF[i], ("tf", i)

    def tb(self):
        i = self.TBr.next()
        return self.TB[i], ("tb", i)

    def tbp(self):
        i = self.TBPr.next()
        return self.TBP[i], ("tbp", i)

    def spawn(self, gen, lo=False, top=False):
        if top:
            self.bg_hi.insert(0, gen)
        else:
            (self.bg_lo if lo else self.bg_hi).append(gen)
        return gen

    def pump(self, k=1, banks=None, lo_ok=True):
        self.bg_banks = banks
        for _ in range(k):
            done = False
            for q in ((self.bg_hi, self.bg_lo) if lo_ok else (self.bg_hi,)):
                while q and not done:
                    try:
                        next(q[0])
                        done = True
                    except StopIteration:
                        q.pop(0)
                if done:
                    break
            if not done:
                break
        self.bg_banks = None

    def finish(self, gen, banks=None):
        while gen in self.bg_hi or gen in self.bg_lo:
            q = self.bg_hi if gen in self.bg_hi else self.bg_lo
            self.bg_banks = banks
            try:
                next(q[0])
            except StopIteration:
                q.pop(0)
        self.bg_banks = None

    def drain_hi(self, banks=None):
        while self.bg_hi:
            self.finish(self.bg_hi[0], banks)

    def load_w(self, src2d, K, W):
        i = self.Wr.next()
        slot = self.WR[i]
        kc = (K + 127) // 128
        assert kc * W <= 1024, (K, W)
        dst = slot[:, 0:kc * W].rearrange("p (c n) -> p c n", n=W)
        if K >= 128:
            srcv = src2d.rearrange("(c p) n -> p c n", p=128)
            self.E.add("pool", lambda e: e.dma_start(out=dst, in_=srcv), writes=[("w", i)], dma=True)
        else:
            srcv = src2d.rearrange("(c p) n -> p c n", p=K)
            d2 = dst[0:K]
            self.E.add("pool", lambda e: e.dma_start(out=d2, in_=srcv), writes=[("w", i)], dma=True)
        return dst, ("w", i)

    def mm(self, out, lhsT, rhs, start, stop, reads, wkey):
        self.E.add("pe", lambda e: e.matmul(out, lhsT, rhs, start=start, stop=stop), reads=reads, writes=[wkey])

    def act(self, out, in_, func, reads, writes, scale=None, bias=None):
        kw = {}
        if scale is not None:
            kw["scale"] = scale
        if bias is not None:
            kw["bias"] = bias
        self.E.add("act", lambda e: e.activation(out, in_, func, **kw), reads=reads, writes=writes)

    def tt(self, out, in0, in1, op, reads, writes, eng="dve"):
        self.E.add(eng, lambda e: e.tensor_tensor(out, in0, in1, op), reads=reads, writes=writes)

    def ts(self, out, in0, s1, s2, op0, op1, reads, writes, eng="dve"):
        if op1 is None:
            self.E.add(eng, lambda e: e.tensor_scalar(out, in0, s1, None, op0), reads=reads, writes=writes)
        else:
            self.E.add(eng, lambda e: e.tensor_scalar(out, in0, s1, s2, op0, op1), reads=reads, writes=writes)

    def stt(self, out, in0, scalar, in1, op0, op1, reads, writes):
        self.E.add("dve", lambda e: e.scalar_tensor_tensor(out, in0, scalar, in1, op0, op1), reads=reads, writes=writes)

    def pl(self, i, t0, sz):
        return self.PL[:, i, t0:t0 + sz]

    def rstd_from_ss(self, ss_ps, ssk, n_feat, sz, to_psum=False):
        t1, t1k = self.tf()
        self.act(t1[:, 0:sz], ss_ps[:, 0:sz], AF.Ln, [ssk, ("c", "eps")], [t1k], scale=1.0 / n_feat, bias=self.epsT[:, 0:1])
        if to_psum:
            t2, t2k = self.psum(self.bg_banks)
        else:
            i = self.RSr.next()
            t2, t2k = self.RS[i], ("rs", i)
        self.act(t2[:, 0:sz], t1[:, 0:sz], AF.Exp, [t1k], [t2k], scale=-0.5)
        return t2, t2k

    def phase_load(self):
        E = self.E
        nc = self.nc
        E.add("sp", lambda e: e.dma_start(out=self.ident[:], in_=self.ident_d), writes=[("c", "ident")], dma=True)
        E.add("sp", lambda e: e.dma_start(out=self.CB[:], in_=self.cb_d), writes=[("c", "cb")], dma=True)
        E.add("dve", lambda e: e.memset(self.epsT[:], EPS), writes=[("c", "eps")])
        nvt = VROWS // 128
        for j in range(nvt):
            st_, stk = self.tf()
            sv = st_[:, 0:128]
            src = self.vecs[j * 128:(j + 1) * 128, :]
            E.add("sp", lambda e, sv=sv, src=src: e.dma_start(out=sv, in_=src), writes=[stk], dma=True)
            ps, psk = self.psum()
            E.add("pe", lambda e, ps=ps, sv=sv: e.transpose(ps[:, 0:128], sv, self.ident[:]),
                  reads=[stk, ("c", "ident")], writes=[psk])
            dst = self.VT[:, j * 128:(j + 1) * 128]
            E.add("dve", lambda e, dst=dst, ps=ps: e.tensor_copy(dst, ps[:, 0:128]), reads=[psk], writes=[("vt",)])
        for j, nm in enumerate(("c", "cctx")):
            o = VOFF[nm]
            self.act(self.condT[:, :, j], self.VT[:, o:o + 8], AF.Silu, [("vt",)], [("cond",)])
        for j in range(NT // 128):
            sl = []
            for half in range(2):
                q = (2 * j + half) % 8
                sl.append((self.FP[:, q // 4, (q % 4) * 512:(q % 4) * 512 + 512], ("stg", q)))
            (st_, stk), (st2, st2k) = sl
            if j < 2:
                src = self.ctx[j * 128:(j + 1) * 128, :]
            else:
                src = self.x[(j - 2) * 128:(j - 1) * 128, :]
            E.add("sp", lambda e, a=st_, src=src: e.dma_start(out=a, in_=src[:, 0:512]), writes=[stk], dma=True)
            E.add("sp", lambda e, a=st2, src=src: e.dma_start(out=a, in_=src[:, 512:1024]), writes=[st2k], dma=True)
            n = self.chunk_of(j * 128)
            for half, (sa, sk) in enumerate(((st_, stk), (st2, st2k))):
                ps, psk = self.psum()
                for q in range(4):
                    E.add("pe", lambda e, ps=ps, sa=sa, q=q: e.transpose(ps[:, q * 128:(q + 1) * 128], sa[:, q * 128:(q + 1) * 128], self.ident[:]),
                          reads=[sk, ("c", "ident")], writes=[psk])
                dst = self.XT[:, half * 4:half * 4 + 4, j * 128:(j + 1) * 128]
                srcp = ps[:, :].rearrange("p (c t) -> p c t", t=128)
                wk = [("x", half * 4 + q, n) for q in range(4)]
                if half == 0:
                    E.add("dve", lambda e, dst=dst, srcp=srcp: e.tensor_copy(dst, srcp), reads=[psk], writes=wk)
                else:
                    E.add("act", lambda e, dst=dst, srcp=srcp: e.activation(dst, srcp, AF.Copy), reads=[psk], writes=wk)
            self.pump(1, None)

    def chunk_of(self, t):
        for n, (t0, sz) in enumerate(TCH):
            if t0 <= t < t0 + sz:
                return n
        raise ValueError

    def gen_mod(self, l):
        ob = VOFF["adab"] + l * 48
        for fc in range(48):
            v = fc // 8
            wt, wk = self.load_w(self.ada_w[l][:, fc * 128:(fc + 1) * 128], D, 128)
            ps, psk = self.psum(self.bg_banks)
            for k in range(8):
                self.mm(ps[:, 0:2], wt[:, k, :], self.condT[:, k, :], k == 0, k == 7, [wk, ("cond",)], psk)
            self.act(self.MOD[:, l, fc, :], ps[:, 0:2], AF.Identity, [psk, ("vt",)], [("mod", l, v)],
                     bias=self.VT[:, ob + fc:ob + fc + 1])
            if fc % 8 == 7 and v in (1, 4):
                sub = 0 if v == 1 else 1
                og = VOFF["n1g" if sub == 0 else "n2g"] + l * 8
                for j in range(2):
                    sc = self.MOD[:, l, v * 8:v * 8 + 8, j]
                    A = self.MP[:, l, sub, 0, :, j]
                    self.stt(A, sc, 1.0, self.VT[:, og:og + 8], ALU.add, ALU.mult, [("mod", l, v), ("vt",)], [("mp", l, sub)])
            self.mod_done[l] = fc + 1
            yield

    def need_mod(self, l, v):
        while self.mod_done[l] < (v + 1) * 8:
            self.pump(1, None)

    def modv(self, l, s, which, c, n):
        j = 1 if n == 0 else 0
        if which == 0:
            return self.MP[:, l, s, 0, c, j:j + 1]
        v = 3 * s + (0 if which == 1 else 2)
        return self.MOD[:, l, v * 8 + c, j:j + 1]

    def modk(self, l, s, which):
        if which == 0:
            return ("mp", l, s)
        return ("mod", l, 3 * s + (0 if which == 1 else 2))

    def phase_norm(self, l, s, chunks, pump=True):
        self.need_mod(l, 3 * s + 1)
        ones = self.CB[:, 0, :]
        for n in chunks:
            if (l, s, n) in self.norm_done:
                continue
            self.norm_done.add((l, s, n))
            t0, sz = TCH[n]
            ss, ssk = self.psum()
            for c in range(8):
                sq, sqk = self.tb()
                if c % 8 not in (1, 4, 6):
                    self.act(sq[:, 0:sz], self.XT[:, c, t0:t0 + sz], AF.Square, [("x", c, n)], [sqk])
                else:
                    xa = self.XT[:, c, t0:t0 + sz]
                    self.tt(sq[:, 0:sz], xa, xa, ALU.mult, [("x", c, n)], [sqk])
                self.mm(ss[:, 0:sz], ones, sq[:, 0:sz], c == 0, c == 7, [sqk, ("c", "cb")], ssk)
            r, rk = self.rstd_from_ss(ss, ssk, D, sz, to_psum=True)
            for c in range(8):
                tmp, tmpk = self.tf()
                self.tt(tmp[:, 0:sz], self.XT[:, c, t0:t0 + sz], r[:, 0:sz], ALU.mult, [("x", c, n), rk], [tmpk])
                self.act(self.pl(c, t0, sz), tmp[:, 0:sz], AF.Identity, [tmpk, self.modk(l, s, 0), self.modk(l, s, 1)], [("pl", c, n)],
                         scale=self.modv(l, s, 0, c, n), bias=self.modv(l, s, 1, c, n))
            if pump:
                self.pump(2, None)

    def tail_split(self, chunks):
        t = self.tail
        if t is None or len(chunks) < 4:
            return [list(chunks)]
        h = 3 if len(chunks) == 5 else 2
        return [list(chunks[:h]), list(chunks[h:])]

    def tail_step(self, first_pass_chunks, d):
        t = self.tail
        if t is None:
            return
        l2, s2, cl2 = t
        todo = [n for n in first_pass_chunks if n in cl2 and (l2, s2, n) not in self.norm_done]
        if todo and d % 2 == 0:
            self.phase_norm(l2, s2, [todo[0]], pump=False)

    def resid_add(self, l, s, d, n, ps, psk):
        t0, sz = TCH[n]
        xa = self.XT[:, d, t0:t0 + sz]
        self.stt(xa, ps[:, 0:sz], self.modv(l, s, 2, d, n), xa, ALU.mult, ALU.add,
                 [psk, ("x", d, n), self.modk(l, s, 2)], [("x", d, n)])

    def phase_ffn(self, l, chunks, late=()):
        groups = [list(range(0, 8)), list(range(8, 15)), list(range(15, 22))]
        GP = 8
        def stage1(fi, f, w, ns):
            w1, w1k, w3, w3k = w
            for n in ns:
                t0, sz = TCH[n]
                p1, p1k = self.psum()
                p3, p3k = self.psum()
                for k in range(8):
                    self.mm(p1[:, 0:sz], w1[:, k, :], self.pl(k, t0, sz), k == 0, k == 7, [w1k, ("pl", k, n)], p1k)
                for k in range(8):
                    self.mm(p3[:, 0:sz], w3[:, k, :], self.pl(k, t0, sz), k == 0, k == 7, [w3k, ("pl", k, n)], p3k)
                s1, s1k = self.tf()
                self.act(s1[:, 0:sz], p1[:, 0:sz], AF.Silu, [p1k], [s1k])
                self.tt(self.pl(GP + fi, t0, sz), s1[:, 0:sz], p3[:, 0:sz], ALU.mult, [s1k, p3k], [("pl", GP + fi, n)])

        def loadw(f):
            w1, w1k = self.load_w(self.ffn_w1[l][:, f * 128:(f + 1) * 128], D, 128)
            w3, w3k = self.load_w(self.ffn_w3[l][:, f * 128:(f + 1) * 128], D, 128)
            return (w1, w1k, w3, w3k)

        hsplit = 3 if len(chunks) == 5 else 2
        for gi, grp in enumerate(groups):
            for fi, f in enumerate(grp):
                if gi == 0 and fi == 0:
                    wa, wb_ = loadw(grp[0]), loadw(grp[1])
                    stage1(0, grp[0], wa, chunks[:hsplit])
                    if late:
                        self.phase_norm(l, 1, list(late[:1]), pump=False)
                    stage1(1, grp[1], wb_, chunks[:hsplit])
                    if late:
                        self.phase_norm(l, 1, list(late[1:]), pump=False)
                    stage1(0, grp[0], wa, chunks[hsplit:])
                    stage1(1, grp[1], wb_, chunks[hsplit:])
                    self.pump(2, None)
                    continue
                if gi == 0 and fi == 1:
                    continue
                stage1(fi, f, loadw(f), chunks)
                self.pump(2, None)
            ng = len(grp)
            f0 = grp[0]
            self.need_mod(l, 5)
            passes = self.tail_split(chunks) if grp is groups[-1] else [list(chunks)]
            for pi, pchunks in enumerate(passes):
                for d in range(8):
                    wts = []
                    r0 = 0
                    while r0 < ng:
                        nr = min(8, ng - r0)
                        wt, wk = self.load_w(self.ffn_w2[l][(f0 + r0) * 128:(f0 + r0 + nr) * 128, d * 128:(d + 1) * 128], nr * 128, 128)
                        for q in range(nr):
                            wts.append((wt[:, q, :], wk))
                        r0 += nr
                    for n in pchunks:
                        t0, sz = TCH[n]
                        ps, psk = self.psum()
                        for fi in range(ng):
                            self.mm(ps[:, 0:sz], wts[fi][0], self.pl(GP + fi, t0, sz), fi == 0, fi == ng - 1,
                                    [wts[fi][1], ("pl", GP + fi, n)], psk)
                        self.resid_add(l, 1, d, n, ps, psk)
                    if pi == 1:
                        self.tail_step(passes[0], d)
                    else:
                        self.pump(1, None)

    def phase_conv(self, l, chunks):
        E = self.E
        j = l // 3
        win = self.conv_w_in[j]
        BZ = 8
        ocw = VOFF["convw"] + j * 24
        UOFF = {0: 1, 1: 259, 2: 259 + 512, 3: 259 + 1024, 4: 259 + 1536}
        for u in range(2):
            for col in (0, 257, 258, 2307):
                a = self.FP[:, u, col:col + 1]
                E.add("dve", lambda e, a=a: e.memset(a, 0.0),
                      writes=[("fpz", u)] + [("fp", u, n) for n in range(5)] + [("stg", q) for q in range(8)])
        W = {}

        def l1_load(c):
            wc, wck = self.load_w(win[:, D + c * 128:D + (c + 1) * 128], D, 128)
            wx, wxk = self.load_w(win[:, 2 * D + c * 128:2 * D + (c + 1) * 128], D, 128)
            wb, wbk = self.load_w(win[:, c * 128:(c + 1) * 128], D, 128)
            W[c] = (wb, wbk, wc, wck, wx, wxk)

        def l1(c, n):
            u = c % 2
            wb, wbk, wc, wck, wx, wxk = W[c]
            t0, sz = TCH[n]
            pc, pck = self.psum()
            px, pxk = self.psum()
            for k in range(8):
                self.mm(pc[:, 0:sz], wc[:, k, :], self.pl(k, t0, sz), k == 0, k == 7, [wck, ("pl", k, n)], pck)
            for k in range(8):
                self.mm(px[:, 0:sz], wx[:, k, :], self.pl(k, t0, sz), k == 0, k == 7, [wxk, ("pl", k, n)], pxk)
            xs, xsk = self.tf()
            self.act(xs[:, 0:sz], px[:, 0:sz], AF.Copy, [pxk], [xsk])
            o = UOFF[n]
            self.tt(self.FP[:, u, o:o + sz], pc[:, 0:sz], xs[:, 0:sz], ALU.mult, [pck, xsk], [("fp", u, n)])

        def l2(c, n):
            u = c % 2
            wb, wbk, wc, wck, wx, wxk = W[c]
            t0, sz = TCH[n]
            pb, pbk = self.psum()
            for k in range(8):
                self.mm(pb[:, 0:sz], wb[:, k, :], self.pl(k, t0, sz), k == 0, k == 7, [wbk, ("pl", k, n)], pbk)
            o = UOFF[n]
            rk = [("fp", u, n), ("fpz", u), ("vt",)]
            if n - 1 in chunks and n - 1 >= 1:
                rk.append(("fp", u, n - 1))
            if n + 1 in chunks and n >= 1:
                rk.append(("fp", u, n + 1))
            a0, a0k = self.tf()
            self.act(a0[:, 0:sz], self.FP[:, u, o - 1:o - 1 + sz], AF.Copy, rk, [a0k],
                     scale=self.VT[:, ocw + 0 + c:ocw + 0 + c + 1])
            a1, a1k = self.tf()
            self.stt(a1[:, 0:sz], self.FP[:, u, o:o + sz], self.VT[:, ocw + 8 + c:ocw + 8 + c + 1], a0[:, 0:sz],
                     ALU.mult, ALU.add, rk + [a0k], [a1k])
            a2, a2k = self.tf()
            self.stt(a2[:, 0:sz], self.FP[:, u, o + 1:o + 1 + sz], self.VT[:, ocw + 16 + c:ocw + 16 + c + 1], a1[:, 0:sz],
                     ALU.mult, ALU.add, rk + [a1k], [a2k])
            self.tt(self.pl(BZ + c, t0, sz), a2[:, 0:sz], pb[:, 0:sz], ALU.mult, [a2k, pbk], [("pl", BZ + c, n)])

        l1_load(0)
        for n in chunks:
            l1(0, n)
        for c in range(8):
            if c + 1 < 8:
                l1_load(c + 1)
            for n in chunks:
                if c + 1 < 8:
                    l1(c + 1, n)
                l2(c, n)
            self.pump(2, None)
        wout = self.conv_w_out[j]
        self.need_mod(l, 2)
        passes = self.tail_split(chunks)
        for pi, pchunks in enumerate(passes):
            for d in range(8):
                wt, wk = self.load_w(wout[:, d * 128:(d + 1) * 128], D, 128)
                for n in pchunks:
                    t0, sz = TCH[n]
                    ps, psk = self.psum()
                    for c in range(8):
                        self.mm(ps[:, 0:sz], wt[:, c, :], self.pl(BZ + c, t0, sz), c == 0, c == 7, [wk, ("pl", BZ + c, n)], psk)
                    self.resid_add(l, 0, d, n, ps, psk)
                if pi == 1:
                    self.tail_step(passes[0], d)
                else:
                    self.pump(1, None)

    def load_rope(self, which):
        E = self.E
        base = 0 if which == "g" else 2
        for u in range(2):
            src = self.rope_d[base + u]
            dst = self.FP[:, u, :]
            E.add("sp", lambda e, dst=dst, src=src: e.dma_start(out=dst, in_=src),
                  writes=[("fp", u, n) for n in range(5)] + [("fpz", u)] + [("stg", q) for q in range(8)], dma=True)

    def rope_apply(self, qn, qnk, np_, Rm, n, dst, dstk):
        t0, sz = TCH[n]
        rp, rpk = self.psum(self.bg_banks)
        self.mm(rp[0:np_, 0:sz], Rm, qn, True, True, [qnk, ("c", "cb")], rpk)
        cos = self.FP[0:np_, 0, t0:t0 + sz]
        sin = self.FP[0:np_, 1, t0:t0 + sz]
        a, ak = self.tf()
        self.tt(a[0:np_, 0:sz], qn, cos, ALU.mult, [qnk, ("fp", 0, n)], [ak])
        b, bk = self.tf()
        self.tt(b[0:np_, 0:sz], rp[0:np_, 0:sz], sin, ALU.mult, [rpk, ("fp", 1, n)], [bk])
        self.tt(dst, a[0:np_, 0:sz], b[0:np_, 0:sz], ALU.add, [ak, bk], [dstk])

    SB = (0, 1, 2, 3)
    OB = (4,)
    SUMB = (5,)
    PB = (6, 7)

    def attn_core(self, qchunks, kparts, qparts, vplane, oplane, scale):
        ones = self.CB[:, 0, :]
        E = self.E
        G, LOOK = 2, 4
        for n in qchunks:
            t0, sz = TCH[n]
            tiles = list(range(2)) if n == 0 else list(range(18))
            accO, accOk = self.psum(self.OB)
            accS, accSk = self.psum(self.SUMB)
            pend = []
            nt = len(tiles)
            st = {"done": 0}

            def issue_s(jt):
                s, sk = self.psum(self.SB)
                tn = self.chunk_of(jt * 128)
                for pi, ((kp, npk), (qp, npq)) in enumerate(zip(kparts, qparts)):
                    self.mm(s[:, 0:sz], self.PL[0:npk, kp, jt * 128:(jt + 1) * 128], self.PL[0:npq, qp, t0:t0 + sz],
                            pi == 0, pi == len(kparts) - 1, [("pl", kp, tn), ("pl", qp, n)], sk)
                p, pk = self.tbp()
                self.act(p[:, 0:sz], s[:, 0:sz], AF.Exp, [sk], [pk], scale=scale)
                pend.append((jt, p, pk))

            def issue_pv():
                jt, p, pk = pend.pop(0)
                first = st["done"] == 0
                last = st["done"] == nt - 1
                st["done"] += 1
                tn = self.chunk_of(jt * 128)
                self.mm(accO[:, 0:sz], self.PL[:, vplane, jt * 128:(jt + 1) * 128], p[:, 0:sz], first, last,
                        [("pl", vplane, tn), pk], accOk)
                self.mm(accS[:, 0:sz], ones, p[:, 0:sz], first, last, [pk, ("c", "cb")], accSk)

            for g0 in range(0, nt, G):
                E.grp_begin("pe")
                for jt in tiles[g0:g0 + G]:
                    issue_s(jt)
                while len(pend) > LOOK:
                    issue_pv()
                E.grp_end()
                self.pump(1, self.PB)
            while pend:
                E.grp_begin("pe")
                for _ in range(min(G, len(pend))):
                    issue_pv()
                E.grp_end()
                self.pump(1, self.PB)
            op_ = self.pl(oplane, t0, sz)
            rc, rck = self.RC, ("rcb",)
            if self.norm_gen is not None:
                self.finish(self.norm_gen, self.PB)
            self.E.add("dve", lambda e, op_=op_, accO=accO, sz=sz: e.tensor_copy(op_, accO[:, 0:sz]), reads=[accOk], writes=[("pl", oplane, n)])
            self.act(rc[:, 0:sz], accS[:, 0:sz], AF.Ln, [accSk], [rck])
            self.norm_gen = self.spawn(self.gen_normalise(op_, ("pl", oplane, n), sz), top=True)

    def gen_normalise(self, op_, opk, sz):
        rc, rck = self.RC, ("rcb",)
        self.act(rc[:, 0:sz], rc[:, 0:sz], AF.Exp, [rck], [rck], scale=-1.0)
        yield
        self.tt(op_, op_, rc[:, 0:sz], ALU.mult, [opk, rck], [opk])
        yield

    def gen_out_proj(self, l, wo, heads_planes, chunks, last=False):
        nh = len(heads_planes)
        h0 = heads_planes[0][0]
        passes = self.tail_split(chunks) if last else [list(chunks)]
        for pi, pchunks in enumerate(passes):
            for d in range(8):
                wt, wk = self.load_w(wo[h0 * 128:(h0 + nh) * 128, d * 128:(d + 1) * 128], nh * 128, 128)
                for n in pchunks:
                    t0, sz = TCH[n]
                    ps, psk = self.psum(self.bg_banks)
                    for i, (h, plane) in enumerate(heads_planes):
                        self.mm(ps[:, 0:sz], wt[:, i, :], self.pl(plane, t0, sz), i == 0, i == nh - 1, [wk, ("pl", plane, n)], psk)
                    self.resid_add(l, 0, d, n, ps, psk)
                    yield
                if last and pi == 1:
                    self.tail_step(passes[0], d)

    def gen_pnr(self, wsrc, gcol, dplane):
        ones = self.CB[:, 0, :]
        R128 = self.CB[:, 1, :]
        wt, wk = self.load_w(wsrc, D, 128)
        stt_ = {}

        def s1(n):
            t0, sz = TCH[n]
            ps, psk = self.psum(self.bg_banks)
            for k in range(8):
                self.mm(ps[:, 0:sz], wt[:, k, :], self.pl(k, t0, sz), k == 0, k == 7, [wk, ("pl", k, n)], psk)
            sq, sqk = self.tb()
            qr, qrk = self.tb()
            self.E.add("dve", lambda e: e.tensor_copy(qr[:, 0:sz], ps[:, 0:sz]), reads=[psk], writes=[qrk])
            self.tt(sq[:, 0:sz], qr[:, 0:sz], qr[:, 0:sz], ALU.mult, [qrk], [sqk])
            stt_[n] = (sq, sqk, qr, qrk)

        def s2(n):
            t0, sz = TCH[n]
            sq, sqk, qr, qrk = stt_[n]
            ss, ssk = self.psum(self.bg_banks)
            self.mm(ss[:, 0:sz], ones, sq[:, 0:sz], True, True, [sqk, ("c", "cb")], ssk)
            r, rk = self.rstd_from_ss(ss, ssk, 128, sz)
            self.stt(qr[:, 0:sz], qr[:, 0:sz], self.VT[:, gcol:gcol + 1], r[:, 0:sz], ALU.mult, ALU.mult,
                     [qrk, rk, ("vt",)], [qrk])

        def s3(n):
            t0, sz = TCH[n]
            sq, sqk, qr, qrk = stt_[n]
            self.rope_apply(qr[:, 0:sz], qrk, 128, R128, n, self.pl(dplane, t0, sz), ("pl", dplane, n))

        for pair in ((0, 1), (2, 3), (4,)):
            for st in (s1, s2, s3):
                for n in pair:
                    st(n)
                    yield
                if len(pair) == 1:
                    yield

    def run_rr(self, gens):
        gens = list(gens)
        while gens:
            for g in list(gens):
                try:
                    next(g)
                except StopIteration:
                    gens.remove(g)

    def phase_gqa(self, l, chunks):
        self.need_mod(l, 2)
        self.load_rope("g")
        KP = (8, 9)
        VP = (10, 11)
        QP = (12, 13, 14, 15)
        scale = 128.0 ** -0.5
        self.bg_banks = None
        def gen_v(g):
            wt, wk = self.load_w(self.gqa_wv[0][:, g * 128:(g + 1) * 128], D, 128)
            for jt in range(18):
                n = self.chunk_of(jt * 128)
                ps, psk = self.psum()
                for k in range(8):
                    self.mm(ps[:, 0:128], self.PL[:, k, jt * 128:(jt + 1) * 128], wt[:, k, :], k == 0, k == 7, [wk, ("pl", k, n)], psk)
                self.E.add("dve", lambda e, a=self.PL[:, VP[g], jt * 128:(jt + 1) * 128], ps=ps: e.tensor_copy(a, ps[:, 0:128]),
                           reads=[psk], writes=[("pl", VP[g], n)])
                yield

        for g in range(2):
            self.run_rr([self.gen_pnr(self.gqa_wk[0][:, g * 128:(g + 1) * 128], VOFF["gkn"], KP[g]), gen_v(g)])
        gq = self.spawn(self.gen_pnr(self.gqa_wq[0][:, 0:128], VOFF["gqn"], QP[0]))
        self.finish(gq, None)
        for h in range(8):
            g = h // 4
            qp = QP[h % 4]
            gq = None
            if h + 1 < 8:
                gq = self.spawn(self.gen_pnr(self.gqa_wq[0][:, (h + 1) * 128:(h + 2) * 128], VOFF["gqn"], QP[(h + 1) % 4]))
            self.attn_core(chunks, [(KP[g], 128)], [(qp, 128)], VP[g], qp, scale)
            if h % 2 == 1:
                self.spawn(self.gen_out_proj(l, self.gqa_wo[0], [(h - 1, QP[(h - 1) % 4]), (h, qp)], chunks, last=(h == 7)))
            if gq is not None:
                self.finish(gq, None)
        self.drain_hi(None)

    def gen_mla_head(self, h, chunks, CQ, CKV):
        par = h % 2
        Kp, Vp, Qn, Qpe = 0 + 4 * par, 1 + 4 * par, 2 + 4 * par, 3 + 4 * par
        R64 = self.CB[0:64, 2, 0:64]
        wuq = self.mla_w_uq[0]
        wukv = self.mla_w_ukv[0]
        wt, wk = self.load_w(wukv[:, h * 256:h * 256 + 128], 256, 128)
        for n in range(5):
            t0, sz = TCH[n]
            ps, psk = self.psum(self.bg_banks)
            for k in range(2):
                self.mm(ps[:, 0:sz], wt[:, k, :], self.pl(CKV[k], t0, sz), k == 0, k == 1, [wk, ("pl", CKV[k], n)], psk)
            self.act(self.pl(Kp, t0, sz), ps[:, 0:sz], AF.Copy, [psk], [("pl", Kp, n)])
            yield
        wt, wk = self.load_w(wukv[:, h * 256 + 128:h * 256 + 256], 256, 128)
        for j0 in range(0, 18, 4):
            nj = min(4, 18 - j0)
            n = self.chunk_of(j0 * 128)
            assert self.chunk_of((j0 + nj - 1) * 128) == n or True
            ps, psk = self.psum(self.bg_banks)
            rks = set()
            for q in range(nj):
                jt = j0 + q
                nn = self.chunk_of(jt * 128)
                rks.add(nn)
                for k in range(2):
                    self.mm(ps[:, q * 128:(q + 1) * 128], self.PL[:, CKV[k], jt * 128:(jt + 1) * 128], wt[:, k, :], k == 0, k == 1,
                            [wk, ("pl", CKV[k], nn)], psk)
            a = self.PL[:, Vp, j0 * 128:(j0 + nj) * 128]
            self.E.add("dve", lambda e, a=a, ps=ps, nj=nj: e.tensor_copy(a, ps[:, 0:nj * 128]),
                       reads=[psk], writes=[("pl", Vp, nn) for nn in sorted(rks)])
            yield
        wt, wk = self.load_w(wuq[:, h * 192:h * 192 + 128], 768, 128)
        for n in chunks:
            t0, sz = TCH[n]
            ps, psk = self.psum(self.bg_banks)
            for k in range(6):
                self.mm(ps[:, 0:sz], wt[:, k, :], self.pl(CQ[k], t0, sz), k == 0, k == 5, [wk, ("pl", CQ[k], n)], psk)
            self.act(self.pl(Qn, t0, sz), ps[:, 0:sz], AF.Copy, [psk], [("pl", Qn, n)])
            yield
        wt, wk = self.load_w(wuq[:, h * 192 + 128:h * 192 + 192], 768, 64)
        qs = {}

        def sa(n):
            t0, sz = TCH[n]
            ps, psk = self.psum(self.bg_banks)
            for k in range(6):
                self.mm(ps[0:64, 0:sz], wt[:, k, :], self.pl(CQ[k], t0, sz), k == 0, k == 5, [wk, ("pl", CQ[k], n)], psk)
            qn, qnk = self.tb()
            self.act(qn[0:64, 0:sz], ps[0:64, 0:sz], AF.Copy, [psk], [qnk])
            qs[n] = (qn, qnk)

        def sb_(n):
            t0, sz = TCH[n]
            qn, qnk = qs[n]
            self.rope_apply(qn[0:64, 0:sz], qnk, 64, R64, n, self.PL[0:64, Qpe, t0:t0 + sz], ("pl", Qpe, n))

        cl = list(chunks)
        for i in range(len(cl) + 2):
            if i < len(cl):
                sa(cl[i])
                yield
            if i >= 2:
                sb_(cl[i - 2])
                yield

    def phase_mla(self, l, chunks):
        ones = self.CB[:, 0, :]
        R64 = self.CB[0:64, 2, 0:64]
        self.need_mod(l, 2)
        self.load_rope("m")
        CQ = list(range(8, 14))
        CKV = (14, 15)
        KPE = 16
        scale = 192.0 ** -0.5
        wdq = self.mla_w_dq[0]
        wdkv = self.mla_w_dkv[0]
        self.bg_banks = None

        def latent(wsrc, ncol_chunks, planes, gbase):
            prev = [None]
            accs = [self.psum((3, 4, 5, 6, 7)) for _ in range(5)]
            for c in range(ncol_chunks):
                wt, wk = self.load_w(wsrc[:, c * 128:(c + 1) * 128], D, 128)
                for n in range(5):
                    t0, sz = TCH[n]
                    ps, psk = self.psum((0, 1, 2))
                    for k in range(8):
                        self.mm(ps[:, 0:sz], wt[:, k, :], self.pl(k, t0, sz), k == 0, k == 7, [wk, ("pl", k, n)], psk)
                    sq, sqk = self.tb()
                    self.act(sq[:, 0:sz], ps[:, 0:sz], AF.Square, [psk], [sqk])
                    self.act(self.pl(planes[c], t0, sz), ps[:, 0:sz], AF.Copy, [psk, ("vt",)], [("pl", planes[c], n)],
                             scale=self.VT[:, gbase + c:gbase + c + 1])
                    if prev[0] is not None:
                        prev[0]()
                    a, ak = accs[n]

                    def _ones(a=a, ak=ak, sq=sq, sqk=sqk, sz=sz, c=c):
                        self.mm(a[:, 0:sz], ones, sq[:, 0:sz], c == 0, c == ncol_chunks - 1, [sqk, ("c", "cb")], ak)
                    prev[0] = _ones
            prev[0]()
            prev[0] = None
            for n in range(5):
                t0, sz = TCH[n]
                a, ak = accs[n]
                r, rk = self.rstd_from_ss(a, ak, ncol_chunks * 128, sz)
                for c in range(ncol_chunks):
                    pa = self.pl(planes[c], t0, sz)
                    self.tt(pa, pa, r[:, 0:sz], ALU.mult, [("pl", planes[c], n), rk], [("pl", planes[c], n)])

        latent(wdq, 6, CQ, VOFF["mqn"])
        latent(wdkv, 2, CKV, VOFF["mkvn"])
        wt, wk = self.load_w(wdkv[:, 256:320], D, 64)
        for n in range(5):
            t0, sz = TCH[n]
            ps, psk = self.psum()
            for k in range(8):
                self.mm(ps[0:64, 0:sz], wt[:, k, :], self.pl(k, t0, sz), k == 0, k == 7, [wk, ("pl", k, n)], psk)
            kn, knk = self.tb()
            self.act(kn[0:64, 0:sz], ps[0:64, 0:sz], AF.Copy, [psk], [knk])
            self.rope_apply(kn[0:64, 0:sz], knk, 64, R64, n, self.PL[0:64, KPE, t0:t0 + sz], ("pl", KPE, n))
        for plane in (KPE, 3, 7):
            a = self.PL[64:128, plane, :]
            self.E.add("dve", lambda e, a=a: e.memset(a, 0.0), writes=[("pl", plane, n) for n in range(5)])
        gh = self.spawn(self.gen_mla_head(0, chunks, CQ, CKV))
        self.finish(gh, None)
        for h in range(8):
            par = h % 2
            Kp, Vp, Qn, Qpe = 0 + 4 * par, 1 + 4 * par, 2 + 4 * par, 3 + 4 * par
            gh = None
            if h + 1 < 8:
                gh = self.spawn(self.gen_mla_head(h + 1, chunks, CQ, CKV))
            self.attn_core(chunks, [(Kp, 128), (KPE, 128)], [(Qn, 128), (Qpe, 128)], Vp, Qn, scale)
            if gh is not None:
                self.finish(gh, None)
            self.spawn(self.gen_out_proj(l, self.mla_wo[0], [(h, Qn)], chunks, last=(h == 7)))
        self.drain_hi(None)

    def phase_final(self):
        E = self.E
        ones = self.CB[:, 0, :]
        og = VOFF["fing"]
        for n in range(1, 5):
            t0, sz = TCH[n]
            if self.final_norm:
                ss, ssk = self.psum()
                for c in range(8):
                    sq, sqk = self.tb()
                    self.act(sq[:, 0:sz], self.XT[:, c, t0:t0 + sz], AF.Square, [("x", c, n)], [sqk])
                    self.mm(ss[:, 0:sz], ones, sq[:, 0:sz], c == 0, c == 7, [sqk, ("c", "cb")], ssk)
                r, rk = self.rstd_from_ss(ss, ssk, D, sz)
            ys = []
            for c in range(8):
                if self.final_norm:
                    y, yk = self.FP[:, 0, (c % 4) * 512:(c % 4) * 512 + 512], ("fy", c % 4)
                    self.stt(y[:, 0:sz], self.XT[:, c, t0:t0 + sz], self.VT[:, og + c:og + c + 1], r[:, 0:sz], ALU.mult, ALU.mult,
                             [("x", c, n), rk, ("vt",)], [yk])
                    ys.append((y, yk, 0))
                else:
                    ys.append((self.XT[:, c, :], ("x", c, n), t0))
                if c % 4 == 3:
                    for tt_ in range(sz // 128):
                        ps, psk = self.psum()
                        for q in range(4):
                            y, yk, yo = ys[q]
                            E.add("pe", lambda e, ps=ps, y=y, q=q, tt_=tt_, yo=yo: e.transpose(
                                ps[:, q * 128:(q + 1) * 128], y[:, yo + tt_ * 128:yo + (tt_ + 1) * 128], self.ident[:]),
                                reads=[yk, ("c", "ident")], writes=[psk])
                        ot, otk = self.tf()
                        if tt_ % 2 == 0:
                            E.add("act", lambda e, ot=ot, ps=ps: e.activation(ot[:, :], ps[:, :], AF.Copy), reads=[psk], writes=[otk])
                        else:
                            E.add("dve", lambda e, ot=ot, ps=ps: e.tensor_copy(ot[:, :], ps[:, :]), reads=[psk], writes=[otk])
                        r0 = t0 - NCTX + tt_ * 128
                        cb = (c // 4) * 512
                        dst = self.out[r0:r0 + 128, cb:cb + 512]
                        E.add("sp", lambda e, dst=dst, ot=ot: e.dma_start(out=dst, in_=ot[:, :]), reads=[otk], dma=True)
                    ys = []

    def build(self):
        nc = self.nc
        self._psrot = {}
        self.bg_hi = []
        self.bg_lo = []
        self.bg_banks = None
        self.norm_gen = None
        self.tail = None
        self.norm_done = set()
        self.mod_done = {l: 0 for l in range(DEPTH)}
        with contextlib.ExitStack() as st:
            sb = lambda name, shape, dtype: st.enter_context(nc.sbuf_tensor(name, shape, dtype))
            self.XT = sb("XT", [128, 8, NT], F32)
            self.PL = sb("PL", [128, NPL, NT], BF16)
            self.FP = sb("FP", [128, 2, UW], F32)
            NW = 7
            self.WR = [sb("WR%d" % i, [128, 1024], BF16) for i in range(NW)]
            self.Wr = Ring("w", NW)
            self.TF = [sb("TF%d" % i, [128, 512], F32) for i in range(3)]
            self.TFr = Ring("tf", 3)
            self.RC = sb("RC", [128, 512], F32)
            self.TB = [sb("TB%d" % i, [128, 512], BF16) for i in range(4)]
            self.TBr = Ring("tb", 4)
            self.TBP = [sb("TBP%d" % i, [128, 512], BF16) for i in range(6)]
            self.TBPr = Ring("tbp", 6)
            self.RS = [sb("RS%d" % i, [128, 512], F32) for i in range(2)]
            self.RSr = Ring("rs", 2)
            self.ident = sb("ident_sb", [128, 128], F32)
            self.CB = sb("CB", [128, 4, 128], BF16)
            self.VT = sb("VT", [128, VROWS], F32)
            self.condT = sb("condT", [128, 8, 2], BF16)
            self.MOD = sb("MOD", [128, DEPTH, 48, 2], F32)
            self.MP = sb("MP", [128, DEPTH, 2, 1, 8, 2], F32)
            self.epsT = sb("epsT", [128, 1], F32)
            self.PS = [st.enter_context(nc.psum_tensor("PS%d" % i, [128, 512], F32)) for i in range(8)]

            if self.layers:
                self.spawn(self.gen_mod(self.layers[0]), lo=True)
            self.phase_load()
            for li, l in enumerate(self.layers):
                if li + 1 < len(self.layers):
                    self.spawn(self.gen_mod(self.layers[li + 1]), lo=True)
                kind = l % 3
                last = (l == DEPTH - 1)
                chunks = [1, 2, 3, 4] if last else [0, 1, 2, 3, 4]
                nchunks = chunks if kind == 0 else [0, 1, 2, 3, 4]
                self.phase_norm(l, 0, nchunks)
                self.tail = (l, 1, chunks)
                if kind == 0:
                    self.phase_conv(l, chunks)
                elif kind == 1:
                    self.phase_gqa(l, chunks)
                else:
                    self.phase_mla(l, chunks)
                self.tail = None
                hs = 3 if len(chunks) == 5 else 2
                self.phase_norm(l, 1, chunks[:hs])
                if li + 1 < len(self.layers):
                    l2 = self.layers[li + 1]
                    last2 = (l2 == DEPTH - 1)
                    ch2 = [1, 2, 3, 4] if last2 else [0, 1, 2, 3, 4]
                    self.tail = (l2, 0, ch2 if l2 % 3 == 0 else [0, 1, 2, 3, 4])
                self.phase_ffn(l, chunks, late=chunks[hs:])
                self.tail = None
            while self.bg_lo or self.bg_hi:
                self.pump(1, None)
            self.E.barrier()
            self.phase_final()
            self.sem_counts = self.E.emit()


_CACHE = {}


def _host_consts():
    if "c" not in _CACHE:
        ident, cb = _consts()
        rt = _rope_tables()
        rope = np.stack([rt["g"][0], rt["g"][1], rt["m"][0], rt["m"][1]]).astype(np.float32)
        _CACHE["c"] = (ident, cb, rope)
    return _CACHE["c"]


def make_in_maps(inputs, cores):
    ident, cb, rope = _host_consts()
    f = lambda a: np.ascontiguousarray(np.asarray(a, dtype=np.float32))
    shared = {}
    for k in ("ada_w", "ffn_w1", "ffn_w3", "ffn_w2", "conv_w_in", "conv_w_out", "gqa_wq", "gqa_wk", "gqa_wv",
              "gqa_wo", "mla_w_dq", "mla_w_uq", "mla_w_dkv", "mla_w_ukv", "mla_wo"):
        shared[k] = f(inputs[k])
    shared["ident"] = ident
    shared["cbf"] = cb
    shared["rope"] = rope
    x = f(inputs["x"])
    c = f(inputs["c"])
    ctx = f(inputs["ctx"])
    maps = []
    for b in cores:
        vecs = np.zeros((VROWS, 128), np.float32)

        def put(name, arr):
            a = f(arr).reshape(-1, 128)
            vecs[VOFF[name]:VOFF[name] + a.shape[0]] = a
        put("c", c[b])
        put("cctx", inputs["c_ctx"])
        put("adab", inputs["ada_b"])
        put("n1g", inputs["norm1_g"])
        put("n2g", inputs["norm2_g"])
        put("convw", inputs["conv_w"])
        put("fing", inputs["final_g"])
        put("gqn", inputs["gqa_q_norm"])
        put("gkn", inputs["gqa_k_norm"])
        put("mqn", inputs["mla_q_norm"])
        put("mkvn", inputs["mla_kv_norm"])
        m = dict(shared)
        m["x"] = x[b]
        m["ctx"] = ctx[b]
        m["vecs"] = vecs
        maps.append(m)
    return maps


def kernel(**inputs):
    if "nc" not in _CACHE:
        _CACHE["nc"] = Builder().nc
    nc = _CACHE["nc"]
    maps = make_in_maps(inputs, list(range(8)))
    res = run_bass_kernel_spmd(nc, maps, core_ids=list(range(8)))
    return np.stack([r["out"] for r in res.results], axis=0).astype(np.float32)
```

```python
import contextlib
import numpy as np
import ml_dtypes
import concourse.bass as bass
import concourse.mybir as mybir
from concourse.bass_utils import run_bass_kernel_spmd

F32 = mybir.dt.float32
BF16 = mybir.dt.bfloat16
AF = mybir.ActivationFunctionType
ALU = mybir.AluOpType

ENGS = ["pe", "act", "dve", "pool", "sp"]

D = 1024
NT = 2304
NCTX = 256
SEQ = 2048
DEPTH = 4
FH = 2816
NFC = 22
EPS = 1e-6
TCH = [(0, 256), (256, 512), (768, 512), (1280, 512), (1792, 512)]
NPL = 17
UW = 2308


class Op:
    __slots__ = ("eng", "fn", "deps", "is_dma", "need_inc", "val", "sem", "name", "seq")

    def __init__(self, eng, fn, is_dma, name):
        self.eng = eng
        self.fn = fn
        self.deps = set()
        self.is_dma = is_dma
        self.need_inc = False
        self.val = None
        self.sem = None
        self.name = name


class Em:
    def __init__(self, nc, n_dma_sems=10):
        self.nc = nc
        self.q = {e: [] for e in ENGS}
        self.lastw = {}
        self.readers = {}
        self.pending = {e: set() for e in ENGS}
        self.last_op = {e: None for e in ENGS}
        self.live_dmas = []
        self.n_dma_sems = n_dma_sems
        self.dma_rr = {e: 0 for e in ENGS}
        self.dma_last = {}
        self.dma_cnt = {}
        self.dma_engs = set()
        self.seq = 0
        self.grp_eng = None
        self.grp_ops = []
        self.grp_start = 0

    def grp_begin(self, eng="pe"):
        self.grp_eng = eng
        self.grp_ops = []
        self.grp_start = self.seq

    def grp_end(self):
        ops = self.grp_ops
        self.grp_eng = None
        self.grp_ops = []
        if len(ops) > 1:
            first = ops[0]
            for o in ops[1:]:
                mv = set(d for d in o.deps if d.seq < self.grp_start)
                first.deps |= mv
                o.deps -= mv

    def add(self, eng, fn, reads=(), writes=(), dma=False, name=""):
        op = Op(eng, fn, dma, name)
        self.seq += 1
        op.seq = self.seq
        if self.grp_eng == eng:
            self.grp_ops.append(op)
        deps = op.deps
        for k in reads:
            w = self.lastw.get(k)
            if w is not None:
                deps.add(w)
            if k[0] == "ps":
                for r in self.readers.get(k, {}).values():
                    if r.eng != eng:
                        deps.add(r)
        for k in writes:
            w = self.lastw.get(k)
            if w is not None and (dma or w.is_dma or w.eng != eng or eng != "pe"):
                deps.add(w)
            for r in self.readers.get(k, {}).values():
                if dma or r.is_dma or r.eng != eng or eng != "pe":
                    deps.add(r)
        if self.pending[eng]:
            deps |= self.pending[eng]
            self.pending[eng] = set()
        if dma:
            self.dma_engs.add(eng)
            slot = self.dma_rr[eng]
            self.dma_rr[eng] = (slot + 1) % self.n_dma_sems
            prev = self.dma_last.get((eng, slot))
            if prev is not None:
                deps.add(prev)
            self.dma_last[(eng, slot)] = op
            op.sem = (eng, slot)
            c = self.dma_cnt.get((eng, slot), 0) + 16
            self.dma_cnt[(eng, slot)] = c
            op.val = c
            self.live_dmas.append(op)
        deps.discard(op)
        for d in deps:
            d.need_inc = True
        rkey = id(op) if dma else eng
        for k in reads:
            self.readers.setdefault(k, {})[rkey] = op
        for k in writes:
            self.lastw[k] = op
            self.readers[k] = {}
        self.q[eng].append(op)
        self.last_op[eng] = op
        return op

    def barrier(self):
        s = set(o for o in self.last_op.values() if o is not None)
        s |= set(self.live_dmas)
        self.live_dmas = []
        for e in ENGS:
            self.pending[e] |= s
        self.lastw = {}
        self.readers = {}

    def emit(self, final_wait_eng="sp"):
        nc = self.nc
        self.barrier()
        self.add(final_wait_eng, lambda e: e.nop(), name="final")
        cnt = {e: 0 for e in ENGS}
        EPOCH = 4000
        for e in ENGS:
            for op in self.q[e]:
                if not op.is_dma:
                    op.sem = (e, "c", cnt[e] // EPOCH)
                    if op.need_inc:
                        op.val = cnt[e] % EPOCH + 1
                        cnt[e] += 1
        with contextlib.ExitStack() as st:
            sems = {}
            for e in ENGS:
                for ep in range(cnt[e] // EPOCH + 1):
                    sems[(e, "c", ep)] = st.enter_context(nc.semaphore("s_%s_%d" % (e, ep)))
            for e in sorted(self.dma_engs):
                for s in range(self.n_dma_sems):
                    sems[(e, s)] = st.enter_context(nc.semaphore("d_%s_%d" % (e, s)))
            block = st.enter_context(nc.Block())

            def run(eng_name):
                def body(eng):
                    seen = {}
                    for op in self.q[eng_name]:
                        need = {}
                        for d in op.deps:
                            if need.get(d.sem, 0) < d.val:
                                need[d.sem] = d.val
                        for sm in sorted(need, key=str):
                            if seen.get(sm, 0) < need[sm]:
                                eng.wait_ge(sems[sm], need[sm])
                                seen[sm] = need[sm]
                        inst = op.fn(eng)
                        if op.is_dma:
                            inst.then_inc(sems[op.sem], 16)
                        elif op.need_inc:
                            inst.then_inc(sems[op.sem], 1)
                return body

            block.tensor(run("pe"))
            block.scalar(run("act"))
            block.vector(run("dve"))
            block.gpsimd(run("pool"))
            block.sync(run("sp"))
        return cnt


class Ring:
    def __init__(self, name, n):
        self.name = name
        self.n = n
        self.i = 0

    def next(self):
        i = self.i
        self.i = (i + 1) % self.n
        return i


def _vec_layout():
    off = {}
    r = 0

    def put(name, n):
        nonlocal r
        off[name] = r
        r += n
    put("c", 8)
    put("cctx", 8)
    put("adab", DEPTH * 48)
    put("n1g", DEPTH * 8)
    put("n2g", DEPTH * 8)
    put("convw", 2 * 3 * 8)
    put("fing", 8)
    put("gqn", 1)
    put("gkn", 1)
    put("mqn", 6)
    put("mkvn", 2)
    tot = ((r + 127) // 128) * 128
    return off, tot


VOFF, VROWS = _vec_layout()


def _rope_tables():
    S = SEQ
    rows = np.repeat(np.arange(S // 64, dtype=np.float32), 64)
    cols = np.tile(np.arange(64, dtype=np.float32), S // 64)

    def ang(rot_dim):
        n = rot_dim // 4
        freqs = (np.float32(10000.0) ** (-np.arange(n, dtype=np.float32) / np.float32(n))).astype(np.float32)
        return np.concatenate([rows[:, None] * freqs, cols[:, None] * freqs], axis=-1).astype(np.float32)

    out = {}
    for nm, rd in (("g", 128), ("m", 64)):
        a = ang(rd).astype(np.float64)
        half = rd // 2
        cos = np.ones((128, UW), np.float32)
        sin = np.zeros((128, UW), np.float32)
        c = np.cos(a).T.astype(np.float32)
        s = np.sin(a).T.astype(np.float32)
        cos[0:half, NCTX:NT] = c
        cos[half:rd, NCTX:NT] = c
        sin[0:half, NCTX:NT] = s
        sin[half:rd, NCTX:NT] = s
        out[nm] = (cos, sin)
    return out


def _consts():
    ident = np.eye(128, dtype=np.float32)
    cb = np.zeros((128, 4, 128), np.float32)
    cb[:, 0, :] = 1.0
    for j in range(128):
        if j < 64:
            cb[j + 64, 1, j] = -1.0
        else:
            cb[j - 64, 1, j] = 1.0
    for j in range(64):
        if j < 32:
            cb[j + 32, 2, j] = -1.0
        else:
            cb[j - 32, 2, j] = 1.0
    cb[:, 3, :] = np.eye(128, dtype=np.float32)
    return ident, cb.astype(ml_dtypes.bfloat16)


class Builder:
    def __init__(self, layers=(0, 1, 2, 3), final_norm=True):
        self.layers = list(layers)
        self.final_norm = final_norm
        nc = bass.Bass("TRN2", target_bir_lowering=False)
        self.nc = nc
        self.E = Em(nc)
        dt = nc.dram_tensor

        def inp(name, shape, dtype=F32):
            return dt(name, list(shape), dtype, kind="ExternalInput").ap()
        self.x = inp("x", [SEQ, D])
        self.ctx = inp("ctx", [NCTX, D])
        self.vecs = inp("vecs", [VROWS, 128])
        self.ident_d = inp("ident", [128, 128])
        self.cb_d = inp("cbf", [128, 4, 128], BF16)
        self.rope_d = inp("rope", [4, 128, UW])
        self.ada_w = inp("ada_w", [DEPTH, D, 6 * D])
        self.ffn_w1 = inp("ffn_w1", [DEPTH, D, FH])
        self.ffn_w3 = inp("ffn_w3", [DEPTH, D, FH])
        self.ffn_w2 = inp("ffn_w2", [DEPTH, FH, D])
        self.conv_w_in = inp("conv_w_in", [2, D, 3 * D])
        self.conv_w_out = inp("conv_w_out", [2, D, D])
        self.gqa_wq = inp("gqa_wq", [1, D, D])
        self.gqa_wk = inp("gqa_wk", [1, D, 256])
        self.gqa_wv = inp("gqa_wv", [1, D, 256])
        self.gqa_wo = inp("gqa_wo", [1, D, D])
        self.mla_w_dq = inp("mla_w_dq", [1, D, 768])
        self.mla_w_uq = inp("mla_w_uq", [1, 768, 1536])
        self.mla_w_dkv = inp("mla_w_dkv", [1, D, 320])
        self.mla_w_ukv = inp("mla_w_ukv", [1, 256, 2048])
        self.mla_wo = inp("mla_wo", [1, D, D])
        self.out = dt("out", [SEQ, D], F32, kind="ExternalOutput").ap()
        self.build()

    def psum(self, banks=None):
        if banks is None:
            banks = range(8)
        key = tuple(banks)
        r = self._psrot.setdefault(key, [0])
        b = key[r[0] % len(key)]
        r[0] += 1
        return self.PS[b], ("ps", b)

    def tf(self):
        i = self.TFr.next()
        return self.TF[i], ("tf", i)

    def tb(self):
        i = self.TBr.next()
        return self.TB[i], ("tb", i)

    def tbp(self):
        i = self.TBPr.next()
        return self.TBP[i], ("tbp", i)

    def spawn(self, gen, lo=False, top=False):
        if top:
            self.bg_hi.insert(0, gen)
        else:
            (self.bg_lo if lo else self.bg_hi).append(gen)
        return gen

    def pump(self, k=1, banks=None, lo_ok=True):
        self.bg_banks = banks
        n_pe = 0
        guard = 0
        while n_pe < k and guard < 12:
            guard += 1
            done = False
            for q in ((self.bg_hi, self.bg_lo) if lo_ok else (self.bg_hi,)):
                while q and not done:
                    try:
                        r = next(q[0])
                        done = True
                        if r != "free":
                            n_pe += 1
                    except StopIteration:
                        q.pop(0)
                if done:
                    break
            if not done:
                break
        self.bg_banks = None

    def finish(self, gen, banks=None):
        while gen in self.bg_hi or gen in self.bg_lo:
            q = self.bg_hi if gen in self.bg_hi else self.bg_lo
            self.bg_banks = banks
            try:
                next(q[0])
            except StopIteration:
                q.pop(0)
        self.bg_banks = None

    def drain_hi(self, banks=None):
        while self.bg_hi:
            self.finish(self.bg_hi[0], banks)

    def load_w(self, src2d, K, W):
        i = self.Wr.next()
        slot = self.WR[i]
        kc = (K + 127) // 128
        assert kc * W <= 1024, (K, W)
        dst = slot[:, 0:kc * W].rearrange("p (c n) -> p c n", n=W)
        if K >= 128:
            srcv = src2d.rearrange("(c p) n -> p c n", p=128)
            self.E.add("pool", lambda e: e.dma_start(out=dst, in_=srcv), writes=[("w", i)], dma=True)
        else:
            srcv = src2d.rearrange("(c p) n -> p c n", p=K)
            d2 = dst[0:K]
            self.E.add("pool", lambda e: e.dma_start(out=d2, in_=srcv), writes=[("w", i)], dma=True)
        return dst, ("w", i)

    def mm(self, out, lhsT, rhs, start, stop, reads, wkey):
        self.E.add("pe", lambda e: e.matmul(out, lhsT, rhs, start=start, stop=stop), reads=reads, writes=[wkey])

    def act(self, out, in_, func, reads, writes, scale=None, bias=None):
        kw = {}
        if scale is not None:
            kw["scale"] = scale
        if bias is not None:
            kw["bias"] = bias
        self.E.add("act", lambda e: e.activation(out, in_, func, **kw), reads=reads, writes=writes)

    def tt(self, out, in0, in1, op, reads, writes, eng="dve"):
        self.E.add(eng, lambda e: e.tensor_tensor(out, in0, in1, op), reads=reads, writes=writes)

    def ts(self, out, in0, s1, s2, op0, op1, reads, writes, eng="dve"):
        if op1 is None:
            self.E.add(eng, lambda e: e.tensor_scalar(out, in0, s1, None, op0), reads=reads, writes=writes)
        else:
            self.E.add(eng, lambda e: e.tensor_scalar(out, in0, s1, s2, op0, op1), reads=reads, writes=writes)

    def stt(self, out, in0, scalar, in1, op0, op1, reads, writes):
        self.E.add("dve", lambda e: e.scalar_tensor_tensor(out, in0, scalar, in1, op0, op1), reads=reads, writes=writes)

    def pl(self, i, t0, sz):
        return self.PL[:, i, t0:t0 + sz]

    def rstd_from_ss(self, ss_ps, ssk, n_feat, sz, to_psum=False):
        t1, t1k = self.tf()
        self.act(t1[:, 0:sz], ss_ps[:, 0:sz], AF.Ln, [ssk, ("c", "eps")], [t1k], scale=1.0 / n_feat, bias=self.epsT[:, 0:1])
        if to_psum:
            t2, t2k = self.psum(self.bg_banks)
        else:
            i = self.RSr.next()
            t2, t2k = self.RS[i], ("rs", i)
        self.act(t2[:, 0:sz], t1[:, 0:sz], AF.Exp, [t1k], [t2k], scale=-0.5)
        return t2, t2k

    def phase_load(self):
        E = self.E
        nc = self.nc
        E.add("sp", lambda e: e.dma_start(out=self.ident[:], in_=self.ident_d), writes=[("c", "ident")], dma=True)
        E.add("sp", lambda e: e.dma_start(out=self.CB[:], in_=self.cb_d), writes=[("c", "cb")], dma=True)
        E.add("dve", lambda e: e.memset(self.epsT[:], EPS), writes=[("c", "eps")])
        nvt = VROWS // 128
        for j in range(nvt):
            st_, stk = self.tf()
            sv = st_[:, 0:128]
            src = self.vecs[j * 128:(j + 1) * 128, :]
            E.add("sp", lambda e, sv=sv, src=src: e.dma_start(out=sv, in_=src), writes=[stk], dma=True)
            ps, psk = self.psum()
            E.add("pe", lambda e, ps=ps, sv=sv: e.transpose(ps[:, 0:128], sv, self.ident[:]),
                  reads=[stk, ("c", "ident")], writes=[psk])
            dst = self.VT[:, j * 128:(j + 1) * 128]
            E.add("dve", lambda e, dst=dst, ps=ps: e.tensor_copy(dst, ps[:, 0:128]), reads=[psk], writes=[("vt",)])
        for j, nm in enumerate(("c", "cctx")):
            o = VOFF[nm]
            self.act(self.condT[:, :, j], self.VT[:, o:o + 8], AF.Silu, [("vt",)], [("cond",)])
        for j in range(NT // 128):
            sl = []
            for half in range(2):
                q = (2 * j + half) % 8
                sl.append((self.FP[:, q // 4, (q % 4) * 512:(q % 4) * 512 + 512], ("stg", q)))
            (st_, stk), (st2, st2k) = sl
            if j < 2:
                src = self.ctx[j * 128:(j + 1) * 128, :]
            else:
                src = self.x[(j - 2) * 128:(j - 1) * 128, :]
            E.add("sp", lambda e, a=st_, src=src: e.dma_start(out=a, in_=src[:, 0:512]), writes=[stk], dma=True)
            E.add("sp", lambda e, a=st2, src=src: e.dma_start(out=a, in_=src[:, 512:1024]), writes=[st2k], dma=True)
            n = self.chunk_of(j * 128)
            for half, (sa, sk) in enumerate(((st_, stk), (st2, st2k))):
                ps, psk = self.psum()
                for q in range(4):
                    E.add("pe", lambda e, ps=ps, sa=sa, q=q: e.transpose(ps[:, q * 128:(q + 1) * 128], sa[:, q * 128:(q + 1) * 128], self.ident[:]),
                          reads=[sk, ("c", "ident")], writes=[psk])
                dst = self.XT[:, half * 4:half * 4 + 4, j * 128:(j + 1) * 128]
                srcp = ps[:, :].rearrange("p (c t) -> p c t", t=128)
                wk = [("x", half * 4 + q, n) for q in range(4)]
                if half == 0:
                    E.add("dve", lambda e, dst=dst, srcp=srcp: e.tensor_copy(dst, srcp), reads=[psk], writes=wk)
                else:
                    E.add("act", lambda e, dst=dst, srcp=srcp: e.activation(dst, srcp, AF.Copy), reads=[psk], writes=wk)
            self.pump(1, None)

    def chunk_of(self, t):
        for n, (t0, sz) in enumerate(TCH):
            if t0 <= t < t0 + sz:
                return n
        raise ValueError

    def gen_mod(self, l):
        ob = VOFF["adab"] + l * 48
        for fc in range(48):
            v = fc // 8
            wt, wk = self.load_w(self.ada_w[l][:, fc * 128:(fc + 1) * 128], D, 128)
            ps, psk = self.psum(self.bg_banks)
            for k in range(8):
                self.mm(ps[:, 0:2], wt[:, k, :], self.condT[:, k, :], k == 0, k == 7, [wk, ("cond",)], psk)
            self.act(self.MOD[:, l, fc, :], ps[:, 0:2], AF.Identity, [psk, ("vt",)], [("mod", l, v)],
                     bias=self.VT[:, ob + fc:ob + fc + 1])
            if fc % 8 == 7 and v in (1, 4):
                sub = 0 if v == 1 else 1
                og = VOFF["n1g" if sub == 0 else "n2g"] + l * 8
                for j in range(2):
                    sc = self.MOD[:, l, v * 8:v * 8 + 8, j]
                    A = self.MP[:, l, sub, 0, :, j]
                    self.stt(A, sc, 1.0, self.VT[:, og:og + 8], ALU.add, ALU.mult, [("mod", l, v), ("vt",)], [("mp", l, sub)])
            self.mod_done[l] = fc + 1
            yield

    def need_mod(self, l, v):
        while self.mod_done[l] < (v + 1) * 8:
            self.pump(1, None)

    def modv(self, l, s, which, c, n):
        j = 1 if n == 0 else 0
        if which == 0:
            return self.MP[:, l, s, 0, c, j:j + 1]
        v = 3 * s + (0 if which == 1 else 2)
        return self.MOD[:, l, v * 8 + c, j:j + 1]

    def modk(self, l, s, which):
        if which == 0:
            return ("mp", l, s)
        return ("mod", l, 3 * s + (0 if which == 1 else 2))

    def phase_norm(self, l, s, chunks, pump=True):
        self.need_mod(l, 3 * s + 1)
        ones = self.CB[:, 0, :]
        for n in chunks:
            if (l, s, n) in self.norm_done:
                continue
            self.norm_done.add((l, s, n))
            t0, sz = TCH[n]
            ss, ssk = self.psum()
            for c in range(8):
                sq, sqk = self.tb()
                if c % 8 not in (1, 4, 6):
                    self.act(sq[:, 0:sz], self.XT[:, c, t0:t0 + sz], AF.Square, [("x", c, n)], [sqk])
                else:
                    xa = self.XT[:, c, t0:t0 + sz]
                    self.tt(sq[:, 0:sz], xa, xa, ALU.mult, [("x", c, n)], [sqk])
                self.mm(ss[:, 0:sz], ones, sq[:, 0:sz], c == 0, c == 7, [sqk, ("c", "cb")], ssk)
            r, rk = self.rstd_from_ss(ss, ssk, D, sz, to_psum=True)
            for c in range(8):
                tmp, tmpk = self.tf()
                self.tt(tmp[:, 0:sz], self.XT[:, c, t0:t0 + sz], r[:, 0:sz], ALU.mult, [("x", c, n), rk], [tmpk])
                self.act(self.pl(c, t0, sz), tmp[:, 0:sz], AF.Identity, [tmpk, self.modk(l, s, 0), self.modk(l, s, 1)], [("pl", c, n)],
                         scale=self.modv(l, s, 0, c, n), bias=self.modv(l, s, 1, c, n))
            if pump:
                self.pump(2, None)

    def tail_split(self, chunks):
        t = self.tail
        if t is None or len(chunks) < 4:
            return [list(chunks)]
        h = 3 if len(chunks) == 5 else 2
        return [list(chunks[:h]), list(chunks[h:])]

    def tail_step(self, first_pass_chunks, d):
        t = self.tail
        if t is None:
            return
        l2, s2, cl2 = t
        todo = [n for n in first_pass_chunks if n in cl2 and (l2, s2, n) not in self.norm_done]
        if todo and d % 2 == 0:
            self.phase_norm(l2, s2, [todo[0]], pump=False)

    def resid_add(self, l, s, d, n, ps, psk):
        t0, sz = TCH[n]
        xa = self.XT[:, d, t0:t0 + sz]
        self.stt(xa, ps[:, 0:sz], self.modv(l, s, 2, d, n), xa, ALU.mult, ALU.add,
                 [psk, ("x", d, n), self.modk(l, s, 2)], [("x", d, n)])

    def phase_ffn(self, l, chunks, late=()):
        groups = [list(range(0, 8)), list(range(8, 15)), list(range(15, 22))]
        GP = 8
        def stage1(fi, f, w, ns):
            w1, w1k, w3, w3k = w
            for n in ns:
                t0, sz = TCH[n]
                p1, p1k = self.psum()
                p3, p3k = self.psum()
                for k in range(8):
                    self.mm(p1[:, 0:sz], w1[:, k, :], self.pl(k, t0, sz), k == 0, k == 7, [w1k, ("pl", k, n)], p1k)
                for k in range(8):
                    self.mm(p3[:, 0:sz], w3[:, k, :], self.pl(k, t0, sz), k == 0, k == 7, [w3k, ("pl", k, n)], p3k)
                s1, s1k = self.tf()
                self.act(s1[:, 0:sz], p1[:, 0:sz], AF.Silu, [p1k], [s1k])
                self.tt(self.pl(GP + fi, t0, sz), s1[:, 0:sz], p3[:, 0:sz], ALU.mult, [s1k, p3k], [("pl", GP + fi, n)])

        def loadw(f):
            w1, w1k = self.load_w(self.ffn_w1[l][:, f * 128:(f + 1) * 128], D, 128)
            w3, w3k = self.load_w(self.ffn_w3[l][:, f * 128:(f + 1) * 128], D, 128)
            return (w1, w1k, w3, w3k)

        hsplit = 3 if len(chunks) == 5 else 2
        for gi, grp in enumerate(groups):
            for fi, f in enumerate(grp):
                if gi == 0 and fi == 0:
                    wa, wb_ = loadw(grp[0]), loadw(grp[1])
                    stage1(0, grp[0], wa, chunks[:hsplit])
                    if late:
                        self.phase_norm(l, 1, list(late[:1]), pump=False)
                    stage1(1, grp[1], wb_, chunks[:hsplit])
                    if late:
                        self.phase_norm(l, 1, list(late[1:]), pump=False)
                    stage1(0, grp[0], wa, chunks[hsplit:])
                    stage1(1, grp[1], wb_, chunks[hsplit:])
                    self.pump(2, None)
                    continue
                if gi == 0 and fi == 1:
                    continue
                stage1(fi, f, loadw(f), chunks)
                self.pump(2, None)
            ng = len(grp)
            f0 = grp[0]
            self.need_mod(l, 5)
            passes = self.tail_split(chunks) if grp is groups[-1] else [list(chunks)]
            for pi, pchunks in enumerate(passes):
                for d in range(8):
                    wts = []
                    r0 = 0
                    while r0 < ng:
                        nr = min(8, ng - r0)
                        wt, wk = self.load_w(self.ffn_w2[l][(f0 + r0) * 128:(f0 + r0 + nr) * 128, d * 128:(d + 1) * 128], nr * 128, 128)
                        for q in range(nr):
                            wts.append((wt[:, q, :], wk))
                        r0 += nr
                    for n in pchunks:
                        t0, sz = TCH[n]
                        ps, psk = self.psum()
                        for fi in range(ng):
                            self.mm(ps[:, 0:sz], wts[fi][0], self.pl(GP + fi, t0, sz), fi == 0, fi == ng - 1,
                                    [wts[fi][1], ("pl", GP + fi, n)], psk)
                        self.resid_add(l, 1, d, n, ps, psk)
                    if pi == 1:
                        self.tail_step(passes[0], d)
                    else:
                        self.pump(1, None)

    def phase_conv(self, l, chunks):
        E = self.E
        j = l // 3
        win = self.conv_w_in[j]
        BZ = 8
        ocw = VOFF["convw"] + j * 24
        UOFF = {0: 1, 1: 259, 2: 259 + 512, 3: 259 + 1024, 4: 259 + 1536}
        for u in range(2):
            for col in (0, 257, 258, 2307):
                a = self.FP[:, u, col:col + 1]
                E.add("dve", lambda e, a=a: e.memset(a, 0.0),
                      writes=[("fpz", u)] + [("fp", u, n) for n in range(5)] + [("stg", q) for q in range(8)])
        W = {}

        def l1_load(c):
            wc, wck = self.load_w(win[:, D + c * 128:D + (c + 1) * 128], D, 128)
            wx, wxk = self.load_w(win[:, 2 * D + c * 128:2 * D + (c + 1) * 128], D, 128)
            wb, wbk = self.load_w(win[:, c * 128:(c + 1) * 128], D, 128)
            W[c] = (wb, wbk, wc, wck, wx, wxk)

        def l1(c, n):
            u = c % 2
            wb, wbk, wc, wck, wx, wxk = W[c]
            t0, sz = TCH[n]
            pc, pck = self.psum()
            px, pxk = self.psum()
            for k in range(8):
                self.mm(pc[:, 0:sz], wc[:, k, :], self.pl(k, t0, sz), k == 0, k == 7, [wck, ("pl", k, n)], pck)
            for k in range(8):
                self.mm(px[:, 0:sz], wx[:, k, :], self.pl(k, t0, sz), k == 0, k == 7, [wxk, ("pl", k, n)], pxk)
            xs, xsk = self.tf()
            self.act(xs[:, 0:sz], px[:, 0:sz], AF.Copy, [pxk], [xsk])
            o = UOFF[n]
            self.tt(self.FP[:, u, o:o + sz], pc[:, 0:sz], xs[:, 0:sz], ALU.mult, [pck, xsk], [("fp", u, n)])

        def l2(c, n):
            u = c % 2
            wb, wbk, wc, wck, wx, wxk = W[c]
            t0, sz = TCH[n]
            pb, pbk = self.psum()
            for k in range(8):
                self.mm(pb[:, 0:sz], wb[:, k, :], self.pl(k, t0, sz), k == 0, k == 7, [wbk, ("pl", k, n)], pbk)
            o = UOFF[n]
            rk = [("fp", u, n), ("fpz", u), ("vt",)]
            if n - 1 in chunks and n - 1 >= 1:
                rk.append(("fp", u, n - 1))
            if n + 1 in chunks and n >= 1:
                rk.append(("fp", u, n + 1))
            a0, a0k = self.tf()
            self.act(a0[:, 0:sz], self.FP[:, u, o - 1:o - 1 + sz], AF.Copy, rk, [a0k],
                     scale=self.VT[:, ocw + 0 + c:ocw + 0 + c + 1])
            a1, a1k = self.tf()
            self.stt(a1[:, 0:sz], self.FP[:, u, o:o + sz], self.VT[:, ocw + 8 + c:ocw + 8 + c + 1], a0[:, 0:sz],
                     ALU.mult, ALU.add, rk + [a0k], [a1k])
            a2, a2k = self.tf()
            self.stt(a2[:, 0:sz], self.FP[:, u, o + 1:o + 1 + sz], self.VT[:, ocw + 16 + c:ocw + 16 + c + 1], a1[:, 0:sz],
                     ALU.mult, ALU.add, rk + [a1k], [a2k])
            self.tt(self.pl(BZ + c, t0, sz), a2[:, 0:sz], pb[:, 0:sz], ALU.mult, [a2k, pbk], [("pl", BZ + c, n)])

        l1_load(0)
        for n in chunks:
            l1(0, n)
        for c in range(8):
            if c + 1 < 8:
                l1_load(c + 1)
            for n in chunks:
                if c + 1 < 8:
                    l1(c + 1, n)
                l2(c, n)
            self.pump(2, None)
        wout = self.conv_w_out[j]
        self.need_mod(l, 2)
        passes = self.tail_split(chunks)
        for pi, pchunks in enumerate(passes):
            for d in range(8):
                wt, wk = self.load_w(wout[:, d * 128:(d + 1) * 128], D, 128)
                for n in pchunks:
                    t0, sz = TCH[n]
                    ps, psk = self.psum()
                    for c in range(8):
                        self.mm(ps[:, 0:sz], wt[:, c, :], self.pl(BZ + c, t0, sz), c == 0, c == 7, [wk, ("pl", BZ + c, n)], psk)
                    self.resid_add(l, 0, d, n, ps, psk)
                if pi == 1:
                    self.tail_step(passes[0], d)
                else:
                    self.pump(1, None)

    def load_rope(self, which):
        E = self.E
        base = 0 if which == "g" else 2
        for u in range(2):
            src = self.rope_d[base + u]
            dst = self.FP[:, u, :]
            E.add("sp", lambda e, dst=dst, src=src: e.dma_start(out=dst, in_=src),
                  writes=[("fp", u, n) for n in range(5)] + [("fpz", u)] + [("stg", q) for q in range(8)], dma=True)

    def rope_apply(self, qn, qnk, np_, Rm, n, dst, dstk):
        t0, sz = TCH[n]
        rp, rpk = self.psum(self.bg_banks)
        self.mm(rp[0:np_, 0:sz], Rm, qn, True, True, [qnk, ("c", "cb")], rpk)
        cos = self.FP[0:np_, 0, t0:t0 + sz]
        sin = self.FP[0:np_, 1, t0:t0 + sz]
        a, ak = self.tf()
        self.tt(a[0:np_, 0:sz], qn, cos, ALU.mult, [qnk, ("fp", 0, n)], [ak])
        b, bk = self.tf()
        self.tt(b[0:np_, 0:sz], rp[0:np_, 0:sz], sin, ALU.mult, [rpk, ("fp", 1, n)], [bk])
        self.tt(dst, a[0:np_, 0:sz], b[0:np_, 0:sz], ALU.add, [ak, bk], [dstk])

    SB = (0, 1, 2, 3)
    OB = (4,)
    SUMB = (5,)
    PB = (6, 7)

    def attn_core(self, qchunks, kparts, qparts, vplane, oplane, scale):
        ones = self.CB[:, 0, :]
        E = self.E
        G, LOOK = 2, 4
        for n in qchunks:
            t0, sz = TCH[n]
            tiles = list(range(2)) if n == 0 else list(range(18))
            accO, accOk = self.psum(self.OB)
            accS, accSk = self.psum(self.SUMB)
            pend = []
            nt = len(tiles)
            st = {"done": 0}

            def issue_s(jt):
                s, sk = self.psum(self.SB)
                tn = self.chunk_of(jt * 128)
                for pi, ((kp, npk), (qp, npq)) in enumerate(zip(kparts, qparts)):
                    self.mm(s[:, 0:sz], self.PL[0:npk, kp, jt * 128:(jt + 1) * 128], self.PL[0:npq, qp, t0:t0 + sz],
                            pi == 0, pi == len(kparts) - 1, [("pl", kp, tn), ("pl", qp, n)], sk)
                p, pk = self.tbp()
                self.act(p[:, 0:sz], s[:, 0:sz], AF.Exp, [sk], [pk], scale=scale)
                pend.append((jt, p, pk))

            def issue_pv():
                jt, p, pk = pend.pop(0)
                first = st["done"] == 0
                last = st["done"] == nt - 1
                st["done"] += 1
                tn = self.chunk_of(jt * 128)
                self.mm(accO[:, 0:sz], self.PL[:, vplane, jt * 128:(jt + 1) * 128], p[:, 0:sz], first, last,
                        [("pl", vplane, tn), pk], accOk)
                self.mm(accS[:, 0:sz], ones, p[:, 0:sz], first, last, [pk, ("c", "cb")], accSk)

            for g0 in range(0, nt, G):
                E.grp_begin("pe")
                for jt in tiles[g0:g0 + G]:
                    issue_s(jt)
                while len(pend) > LOOK:
                    issue_pv()
                E.grp_end()
                self.pump(1, self.PB)
            while pend:
                E.grp_begin("pe")
                for _ in range(min(G, len(pend))):
                    issue_pv()
                E.grp_end()
                self.pump(1, self.PB)
            op_ = self.pl(oplane, t0, sz)
            rc, rck = self.RC, ("rcb",)
            if self.norm_gen is not None:
                self.finish(self.norm_gen, self.PB)
            self.E.add("dve", lambda e, op_=op_, accO=accO, sz=sz: e.tensor_copy(op_, accO[:, 0:sz]), reads=[accOk], writes=[("pl", oplane, n)])
            self.act(rc[:, 0:sz], accS[:, 0:sz], AF.Ln, [accSk], [rck])
            self.norm_gen = self.spawn(self.gen_normalise(op_, ("pl", oplane, n), sz), top=True)

    def gen_normalise(self, op_, opk, sz):
        rc, rck = self.RC, ("rcb",)
        self.act(rc[:, 0:sz], rc[:, 0:sz], AF.Exp, [rck], [rck], scale=-1.0)
        yield "free"
        self.tt(op_, op_, rc[:, 0:sz], ALU.mult, [opk, rck], [opk])
        yield "free"

    def gen_out_proj(self, l, wo, heads_planes, chunks, last=False):
        nh = len(heads_planes)
        h0 = heads_planes[0][0]
        passes = self.tail_split(chunks) if last else [list(chunks)]
        for pi, pchunks in enumerate(passes):
            for d in range(8):
                wt, wk = self.load_w(wo[h0 * 128:(h0 + nh) * 128, d * 128:(d + 1) * 128], nh * 128, 128)
                for n in pchunks:
                    t0, sz = TCH[n]
                    ps, psk = self.psum(self.bg_banks)
                    for i, (h, plane) in enumerate(heads_planes):
                        self.mm(ps[:, 0:sz], wt[:, i, :], self.pl(plane, t0, sz), i == 0, i == nh - 1, [wk, ("pl", plane, n)], psk)
                    self.resid_add(l, 0, d, n, ps, psk)
                    yield
                if last and pi == 1:
                    self.tail_step(passes[0], d)

    def gen_pnr(self, wsrc, gcol, dplane):
        ones = self.CB[:, 0, :]
        R128 = self.CB[:, 1, :]
        wt, wk = self.load_w(wsrc, D, 128)
        stt_ = {}

        def s1(n):
            t0, sz = TCH[n]
            ps, psk = self.psum(self.bg_banks)
            for k in range(8):
                self.mm(ps[:, 0:sz], wt[:, k, :], self.pl(k, t0, sz), k == 0, k == 7, [wk, ("pl", k, n)], psk)
            sq, sqk = self.tb()
            qr, qrk = self.tb()
            self.E.add("dve", lambda e: e.tensor_copy(qr[:, 0:sz], ps[:, 0:sz]), reads=[psk], writes=[qrk])
            self.tt(sq[:, 0:sz], qr[:, 0:sz], qr[:, 0:sz], ALU.mult, [qrk], [sqk])
            stt_[n] = (sq, sqk, qr, qrk)

        def s2(n):
            t0, sz = TCH[n]
            sq, sqk, qr, qrk = stt_[n]
            ss, ssk = self.psum(self.bg_banks)
            self.mm(ss[:, 0:sz], ones, sq[:, 0:sz], True, True, [sqk, ("c", "cb")], ssk)
            r, rk = self.rstd_from_ss(ss, ssk, 128, sz)
            self.stt(qr[:, 0:sz], qr[:, 0:sz], self.VT[:, gcol:gcol + 1], r[:, 0:sz], ALU.mult, ALU.mult,
                     [qrk, rk, ("vt",)], [qrk])

        def s3(n):
            t0, sz = TCH[n]
            sq, sqk, qr, qrk = stt_[n]
            self.rope_apply(qr[:, 0:sz], qrk, 128, R128, n, self.pl(dplane, t0, sz), ("pl", dplane, n))

        for pair in ((0, 1), (2, 3), (4,)):
            for st in (s1, s2, s3):
                for n in pair:
                    st(n)
                    yield
                if len(pair) == 1:
                    yield

    def run_rr(self, gens):
        gens = list(gens)
        while gens:
            for g in list(gens):
                try:
                    next(g)
                except StopIteration:
                    gens.remove(g)

    def phase_gqa(self, l, chunks):
        self.need_mod(l, 2)
        self.load_rope("g")
        KP = (8, 9)
        VP = (10, 11)
        QP = (12, 13, 14, 15)
        scale = 128.0 ** -0.5
        self.bg_banks = None
        def gen_v(g):
            wt, wk = self.load_w(self.gqa_wv[0][:, g * 128:(g + 1) * 128], D, 128)
            for jt in range(18):
                n = self.chunk_of(jt * 128)
                ps, psk = self.psum()
                for k in range(8):
                    self.mm(ps[:, 0:128], self.PL[:, k, jt * 128:(jt + 1) * 128], wt[:, k, :], k == 0, k == 7, [wk, ("pl", k, n)], psk)
                self.E.add("dve", lambda e, a=self.PL[:, VP[g], jt * 128:(jt + 1) * 128], ps=ps: e.tensor_copy(a, ps[:, 0:128]),
                           reads=[psk], writes=[("pl", VP[g], n)])
                yield

        for g in range(2):
            self.run_rr([self.gen_pnr(self.gqa_wk[0][:, g * 128:(g + 1) * 128], VOFF["gkn"], KP[g]), gen_v(g)])
        gq = self.spawn(self.gen_pnr(self.gqa_wq[0][:, 0:128], VOFF["gqn"], QP[0]))
        self.finish(gq, None)
        for h in range(8):
            g = h // 4
            qp = QP[h % 4]
            gq = None
            if h + 1 < 8:
                gq = self.spawn(self.gen_pnr(self.gqa_wq[0][:, (h + 1) * 128:(h + 2) * 128], VOFF["gqn"], QP[(h + 1) % 4]))
            self.attn_core(chunks, [(KP[g], 128)], [(qp, 128)], VP[g], qp, scale)
            if h % 2 == 1:
                self.spawn(self.gen_out_proj(l, self.gqa_wo[0], [(h - 1, QP[(h - 1) % 4]), (h, qp)], chunks, last=(h == 7)))
            if gq is not None:
                self.finish(gq, None)
        self.drain_hi(None)

    def gen_mla_head(self, h, chunks, CQ, CKV):
        par = h % 2
        Kp, Vp, Qn, Qpe = 0 + 4 * par, 1 + 4 * par, 2 + 4 * par, 3 + 4 * par
        R64 = self.CB[0:64, 2, 0:64]
        wuq = self.mla_w_uq[0]
        wukv = self.mla_w_ukv[0]
        wt, wk = self.load_w(wukv[:, h * 256:h * 256 + 128], 256, 128)
        for n in range(5):
            t0, sz = TCH[n]
            ps, psk = self.psum(self.bg_banks)
            for k in range(2):
                self.mm(ps[:, 0:sz], wt[:, k, :], self.pl(CKV[k], t0, sz), k == 0, k == 1, [wk, ("pl", CKV[k], n)], psk)
            self.act(self.pl(Kp, t0, sz), ps[:, 0:sz], AF.Copy, [psk], [("pl", Kp, n)])
            yield
        wt, wk = self.load_w(wukv[:, h * 256 + 128:h * 256 + 256], 256, 128)
        for j0 in range(0, 18, 4):
            nj = min(4, 18 - j0)
            n = self.chunk_of(j0 * 128)
            assert self.chunk_of((j0 + nj - 1) * 128) == n or True
            ps, psk = self.psum(self.bg_banks)
            rks = set()
            for q in range(nj):
                jt = j0 + q
                nn = self.chunk_of(jt * 128)
                rks.add(nn)
                for k in range(2):
                    self.mm(ps[:, q * 128:(q + 1) * 128], self.PL[:, CKV[k], jt * 128:(jt + 1) * 128], wt[:, k, :], k == 0, k == 1,
                            [wk, ("pl", CKV[k], nn)], psk)
            a = self.PL[:, Vp, j0 * 128:(j0 + nj) * 128]
            self.E.add("dve", lambda e, a=a, ps=ps, nj=nj: e.tensor_copy(a, ps[:, 0:nj * 128]),
                       reads=[psk], writes=[("pl", Vp, nn) for nn in sorted(rks)])
            yield
        wt, wk = self.load_w(wuq[:, h * 192:h * 192 + 128], 768, 128)
        for n in chunks:
            t0, sz = TCH[n]
            ps, psk = self.psum(self.bg_banks)
            for k in range(6):
                self.mm(ps[:, 0:sz], wt[:, k, :], self.pl(CQ[k], t0, sz), k == 0, k == 5, [wk, ("pl", CQ[k], n)], psk)
            self.act(self.pl(Qn, t0, sz), ps[:, 0:sz], AF.Copy, [psk], [("pl", Qn, n)])
            yield
        wt, wk = self.load_w(wuq[:, h * 192 + 128:h * 192 + 192], 768, 64)
        qs = {}

        def sa(n):
            t0, sz = TCH[n]
            ps, psk = self.psum(self.bg_banks)
            for k in range(6):
                self.mm(ps[0:64, 0:sz], wt[:, k, :], self.pl(CQ[k], t0, sz), k == 0, k == 5, [wk, ("pl", CQ[k], n)], psk)
            qn, qnk = self.tb()
            self.act(qn[0:64, 0:sz], ps[0:64, 0:sz], AF.Copy, [psk], [qnk])
            qs[n] = (qn, qnk)

        def sb_(n):
            t0, sz = TCH[n]
            qn, qnk = qs[n]
            self.rope_apply(qn[0:64, 0:sz], qnk, 64, R64, n, self.PL[0:64, Qpe, t0:t0 + sz], ("pl", Qpe, n))

        cl = list(chunks)
        for i in range(len(cl) + 2):
            if i < len(cl):
                sa(cl[i])
                yield
            if i >= 2:
                sb_(cl[i - 2])
                yield

    def phase_mla(self, l, chunks):
        ones = self.CB[:, 0, :]
        R64 = self.CB[0:64, 2, 0:64]
        self.need_mod(l, 2)
        self.load_rope("m")
        CQ = list(range(8, 14))
        CKV = (14, 15)
        KPE = 16
        scale = 192.0 ** -0.5
        wdq = self.mla_w_dq[0]
        wdkv = self.mla_w_dkv[0]
        self.bg_banks = None

        def latent(wsrc, ncol_chunks, planes, gbase):
            prev = [None]
            accs = [self.psum((3, 4, 5, 6, 7)) for _ in range(5)]
            for c in range(ncol_chunks):
                wt, wk = self.load_w(wsrc[:, c * 128:(c + 1) * 128], D, 128)
                for n in range(5):
                    t0, sz = TCH[n]
                    ps, psk = self.psum((0, 1, 2))
                    for k in range(8):
                        self.mm(ps[:, 0:sz], wt[:, k, :], self.pl(k, t0, sz), k == 0, k == 7, [wk, ("pl", k, n)], psk)
                    sq, sqk = self.tb()
                    self.act(sq[:, 0:sz], ps[:, 0:sz], AF.Square, [psk], [sqk])
                    self.act(self.pl(planes[c], t0, sz), ps[:, 0:sz], AF.Copy, [psk, ("vt",)], [("pl", planes[c], n)],
                             scale=self.VT[:, gbase + c:gbase + c + 1])
                    if prev[0] is not None:
                        prev[0]()
                    a, ak = accs[n]

                    def _ones(a=a, ak=ak, sq=sq, sqk=sqk, sz=sz, c=c):
                        self.mm(a[:, 0:sz], ones, sq[:, 0:sz], c == 0, c == ncol_chunks - 1, [sqk, ("c", "cb")], ak)
                    prev[0] = _ones
            prev[0]()
            prev[0] = None
            for n in range(5):
                t0, sz = TCH[n]
                a, ak = accs[n]
                r, rk = self.rstd_from_ss(a, ak, ncol_chunks * 128, sz)
                for c in range(ncol_chunks):
                    pa = self.pl(planes[c], t0, sz)
                    self.tt(pa, pa, r[:, 0:sz], ALU.mult, [("pl", planes[c], n), rk], [("pl", planes[c], n)])

        latent(wdq, 6, CQ, VOFF["mqn"])
        latent(wdkv, 2, CKV, VOFF["mkvn"])
        wt, wk = self.load_w(wdkv[:, 256:320], D, 64)
        for n in range(5):
            t0, sz = TCH[n]
            ps, psk = self.psum()
            for k in range(8):
                self.mm(ps[0:64, 0:sz], wt[:, k, :], self.pl(k, t0, sz), k == 0, k == 7, [wk, ("pl", k, n)], psk)
            kn, knk = self.tb()
            self.act(kn[0:64, 0:sz], ps[0:64, 0:sz], AF.Copy, [psk], [knk])
            self.rope_apply(kn[0:64, 0:sz], knk, 64, R64, n, self.PL[0:64, KPE, t0:t0 + sz], ("pl", KPE, n))
        for plane in (KPE, 3, 7):
            a = self.PL[64:128, plane, :]
            self.E.add("dve", lambda e, a=a: e.memset(a, 0.0), writes=[("pl", plane, n) for n in range(5)])
        gh = self.spawn(self.gen_mla_head(0, chunks, CQ, CKV))
        self.finish(gh, None)
        for h in range(8):
            par = h % 2
            Kp, Vp, Qn, Qpe = 0 + 4 * par, 1 + 4 * par, 2 + 4 * par, 3 + 4 * par
            gh = None
            if h + 1 < 8:
                gh = self.spawn(self.gen_mla_head(h + 1, chunks, CQ, CKV))
            self.attn_core(chunks, [(Kp, 128), (KPE, 128)], [(Qn, 128), (Qpe, 128)], Vp, Qn, scale)
            if gh is not None:
                self.finish(gh, None)
            self.spawn(self.gen_out_proj(l, self.mla_wo[0], [(h, Qn)], chunks, last=(h == 7)))
        self.drain_hi(None)

    def phase_final(self):
        E = self.E
        ones = self.CB[:, 0, :]
        og = VOFF["fing"]
        for n in range(1, 5):
            t0, sz = TCH[n]
            if self.final_norm:
                ss, ssk = self.psum()
                for c in range(8):
                    sq, sqk = self.tb()
                    self.act(sq[:, 0:sz], self.XT[:, c, t0:t0 + sz], AF.Square, [("x", c, n)], [sqk])
                    self.mm(ss[:, 0:sz], ones, sq[:, 0:sz], c == 0, c == 7, [sqk, ("c", "cb")], ssk)
                r, rk = self.rstd_from_ss(ss, ssk, D, sz)
            ys = []
            for c in range(8):
                if self.final_norm:
                    y, yk = self.FP[:, 0, (c % 4) * 512:(c % 4) * 512 + 512], ("fy", c % 4)
                    self.stt(y[:, 0:sz], self.XT[:, c, t0:t0 + sz], self.VT[:, og + c:og + c + 1], r[:, 0:sz], ALU.mult, ALU.mult,
                             [("x", c, n), rk, ("vt",)], [yk])
                    ys.append((y, yk, 0))
                else:
                    ys.append((self.XT[:, c, :], ("x", c, n), t0))
                if c % 4 == 3:
                    for tt_ in range(sz // 128):
                        ps, psk = self.psum()
                        for q in range(4):
                            y, yk, yo = ys[q]
                            E.add("pe", lambda e, ps=ps, y=y, q=q, tt_=tt_, yo=yo: e.transpose(
                                ps[:, q * 128:(q + 1) * 128], y[:, yo + tt_ * 128:yo + (tt_ + 1) * 128], self.ident[:]),
                                reads=[yk, ("c", "ident")], writes=[psk])
                        ot, otk = self.tf()
                        if tt_ % 2 == 0:
                            E.add("act", lambda e, ot=ot, ps=ps: e.activation(ot[:, :], ps[:, :], AF.Copy), reads=[psk], writes=[otk])
                        else:
                            E.add("dve", lambda e, ot=ot, ps=ps: e.tensor_copy(ot[:, :], ps[:, :]), reads=[psk], writes=[otk])
                        r0 = t0 - NCTX + tt_ * 128
                        cb = (c // 4) * 512
                        dst = self.out[r0:r0 + 128, cb:cb + 512]
                        E.add("sp", lambda e, dst=dst, ot=ot: e.dma_start(out=dst, in_=ot[:, :]), reads=[otk], dma=True)
                    ys = []

    def build(self):
        nc = self.nc
        self._psrot = {}
        self.bg_hi = []
        self.bg_lo = []
        self.bg_banks = None
        self.norm_gen = None
        self.tail = None
        self.norm_done = set()
        self.mod_done = {l: 0 for l in range(DEPTH)}
        with contextlib.ExitStack() as st:
            sb = lambda name, shape, dtype: st.enter_context(nc.sbuf_tensor(name, shape, dtype))
            self.XT = sb("XT", [128, 8, NT], F32)
            self.PL = sb("PL", [128, NPL, NT], BF16)
            self.FP = sb("FP", [128, 2, UW], F32)
            NW = 7
            self.WR = [sb("WR%d" % i, [128, 1024], BF16) for i in range(NW)]
            self.Wr = Ring("w", NW)
            self.TF = [sb("TF%d" % i, [128, 512], F32) for i in range(3)]
            self.TFr = Ring("tf", 3)
            self.RC = sb("RC", [128, 512], F32)
            self.TB = [sb("TB%d" % i, [128, 512], BF16) for i in range(4)]
            self.TBr = Ring("tb", 4)
            self.TBP = [sb("TBP%d" % i, [128, 512], BF16) for i in range(6)]
            self.TBPr = Ring("tbp", 6)
            self.RS = [sb("RS%d" % i, [128, 512], F32) for i in range(2)]
            self.RSr = Ring("rs", 2)
            self.ident = sb("ident_sb", [128, 128], F32)
            self.CB = sb("CB", [128, 4, 128], BF16)
            self.VT = sb("VT", [128, VROWS], F32)
            self.condT = sb("condT", [128, 8, 2], BF16)
            self.MOD = sb("MOD", [128, DEPTH, 48, 2], F32)
            self.MP = sb("MP", [128, DEPTH, 2, 1, 8, 2], F32)
            self.epsT = sb("epsT", [128, 1], F32)
            self.PS = [st.enter_context(nc.psum_tensor("PS%d" % i, [128, 512], F32)) for i in range(8)]

            if self.layers:
                self.spawn(self.gen_mod(self.layers[0]), lo=True)
            self.phase_load()
            for li, l in enumerate(self.layers):
                if li + 1 < len(self.layers):
                    self.spawn(self.gen_mod(self.layers[li + 1]), lo=True)
                kind = l % 3
                last = (l == DEPTH - 1)
                chunks = [1, 2, 3, 4] if last else [0, 1, 2, 3, 4]
                nchunks = chunks if kind == 0 else [0, 1, 2, 3, 4]
                self.phase_norm(l, 0, nchunks)
                self.tail = (l, 1, chunks)
                if kind == 0:
                    self.phase_conv(l, chunks)
                elif kind == 1:
                    self.phase_gqa(l, chunks)
                else:
                    self.phase_mla(l, chunks)
                self.tail = None
                hs = 3 if len(chunks) == 5 else 2
                self.phase_norm(l, 1, chunks[:hs])
                if li + 1 < len(self.layers):
                    l2 = self.layers[li + 1]
                    last2 = (l2 == DEPTH - 1)
                    ch2 = [1, 2, 3, 4] if last2 else [0, 1, 2, 3, 4]
                    self.tail = (l2, 0, ch2 if l2 % 3 == 0 else [0, 1, 2, 3, 4])
                self.phase_ffn(l, chunks, late=chunks[hs:])
                self.tail = None
            while self.bg_lo or self.bg_hi:
                self.pump(1, None)
            self.E.barrier()
            self.phase_final()
            self.sem_counts = self.E.emit()


_CACHE = {}


def _host_consts():
    if "c" not in _CACHE:
        ident, cb = _consts()
        rt = _rope_tables()
        rope = np.stack([rt["g"][0], rt["g"][1], rt["m"][0], rt["m"][1]]).astype(np.float32)
        _CACHE["c"] = (ident, cb, rope)
    return _CACHE["c"]


def make_in_maps(inputs, cores):
    ident, cb, rope = _host_consts()
    f = lambda a: np.ascontiguousarray(np.asarray(a, dtype=np.float32))
    shared = {}
    for k in ("ada_w", "ffn_w1", "ffn_w3", "ffn_w2", "conv_w_in", "conv_w_out", "gqa_wq", "gqa_wk", "gqa_wv",
              "gqa_wo", "mla_w_dq", "mla_w_uq", "mla_w_dkv", "mla_w_ukv", "mla_wo"):
        shared[k] = f(inputs[k])
    shared["ident"] = ident
    shared["cbf"] = cb
    shared["rope"] = rope
    x = f(inputs["x"])
    c = f(inputs["c"])
    ctx = f(inputs["ctx"])
    maps = []
    for b in cores:
        vecs = np.zeros((VROWS, 128), np.float32)

        def put(name, arr):
            a = f(arr).reshape(-1, 128)
            vecs[VOFF[name]:VOFF[name] + a.shape[0]] = a
        put("c", c[b])
        put("cctx", inputs["c_ctx"])
        put("adab", inputs["ada_b"])
        put("n1g", inputs["norm1_g"])
        put("n2g", inputs["norm2_g"])
        put("convw", inputs["conv_w"])
        put("fing", inputs["final_g"])
        put("gqn", inputs["gqa_q_norm"])
        put("gkn", inputs["gqa_k_norm"])
        put("mqn", inputs["mla_q_norm"])
        put("mkvn", inputs["mla_kv_norm"])
        m = dict(shared)
        m["x"] = x[b]
        m["ctx"] = ctx[b]
        m["vecs"] = vecs
        maps.append(m)
    return maps


def kernel(**inputs):
    if "nc" not in _CACHE:
        _CACHE["nc"] = Builder().nc
    nc = _CACHE["nc"]
    maps = make_in_maps(inputs, list(range(8)))
    res = run_bass_kernel_spmd(nc, maps, core_ids=list(range(8)))
    return np.stack([r["out"] for r in res.results], axis=0).astype(np.float32)
```
